# Optimizing a Trainium2 kernel written in Bass

```python
import functools
import jax
import jax.numpy as jnp
from jax import lax
import numpy as np

D_MODEL = 2048
BATCH = 32
SEQ = 256
DEPTH = 4
DEC_BATCH = 4
DEC_SEQ = 1024
PAST_LEN = 512

GRID_W = 64
D_ATTN = D_MODEL // 2
N_HEADS = 16
HEAD_DIM = D_ATTN // N_HEADS
WIN_H = 8
WIN_W = 16
D_CONV = D_MODEL // 4
CONV_WIDTH = 31
D_POOL = D_MODEL // 4
POOL_WINDOWS = (2, 4, 8, 16)
POOL_GROUP = D_POOL // len(POOL_WINDOWS)
D_FF = ((8 * D_MODEL // 3 + 127) // 128) * 128
N_MOD = 9
Q_BLOCK = 128
EPS = 1e-6
IN_WIDTH = 3 * D_ATTN + 2 * D_CONV + D_POOL + 3 * D_MODEL
IN_SPLITS = (D_ATTN, 2 * D_ATTN, 3 * D_ATTN, 3 * D_ATTN + 2 * D_CONV, 3 * D_ATTN + 2 * D_CONV + D_POOL)

kernel_name = 'hybrid_na_conv_pool_flow_step'


def rmsnorm(x, g):
    xf = x.astype(jnp.float32)
    y = xf * lax.rsqrt(jnp.mean(xf * xf, axis=-1, keepdims=True) + EPS)
    return (y * g.astype(jnp.float32)).astype(x.dtype)


def swiglu(h, w1, w3, w2):
    return (jax.nn.silu(h @ w1) * (h @ w3)) @ w2


def context_attention(q, k, v):
    B, S, H, Dh = q.shape
    nb = S // Q_BLOCK
    qb = (q * (HEAD_DIM ** -0.5)).reshape(B, nb, Q_BLOCK, H, Dh).transpose(1, 0, 2, 3, 4)

    def block(qi):
        s = jnp.einsum('bqhd,bkhd->bhqk', qi, k).astype(jnp.float32)
        pr = jax.nn.softmax(s, axis=-1).astype(v.dtype)
        return jnp.einsum('bhqk,bkhd->bqhd', pr, v)

    out = lax.map(block, qb)
    return out.transpose(1, 0, 2, 3, 4).reshape(B, S, H, Dh)


def latent_attention(q, k, v, ck, cv, rpb):
    B, N, H, Dh = q.shape
    rows = N // GRID_W
    kh = min(WIN_H, rows)
    nb = rows // 2
    r = jnp.arange(rows)
    row_idx = jnp.clip(r - kh // 2, 0, rows - kh)[:, None] + jnp.arange(kh)[None, :]
    dr = row_idx - r[:, None]
    col = jnp.arange(GRID_W)
    col_start = jnp.clip(col - WIN_W // 2, 0, GRID_W - WIN_W)
    col_in = (col[None, :] >= col_start[:, None]) & (col[None, :] < col_start[:, None] + WIN_W)
    dc_idx = jnp.clip(col[None, :] - col[:, None], -(WIN_W - 1), WIN_W - 1) + (WIN_W - 1)
    kg = k.reshape(B, rows, GRID_W, H, Dh)
    vg = v.reshape(B, rows, GRID_W, H, Dh)
    qb_all = (q * (HEAD_DIM ** -0.5)).reshape(B, nb, 2, GRID_W, H, Dh).transpose(1, 0, 2, 3, 4, 5)

    def block(args):
        qb, bi = args
        rb = 2 * bi + jnp.arange(2)
        idx = row_idx[rb]
        kb = kg[:, idx]
        vb = vg[:, idx]
        bias = rpb[:, dr[rb] + (WIN_H - 1)][..., dc_idx]
        bias = bias.transpose(0, 1, 3, 2, 4).astype(jnp.float32)
        s_loc = jnp.einsum('bpqhd,bpkwhd->bhpqkw', qb, kb).astype(jnp.float32) + bias
        s_loc = jnp.where(col_in[:, None, :], s_loc, -jnp.inf).reshape(B, H, 2, GRID_W, kh * GRID_W)
        s_ctx = jnp.einsum('bpqhd,bkhd->bhpqk', qb, ck).astype(jnp.float32)
        pr = jax.nn.softmax(jnp.concatenate([s_loc, s_ctx], axis=-1), axis=-1).astype(v.dtype)
        p_loc = pr[..., :kh * GRID_W].reshape(B, H, 2, GRID_W, kh, GRID_W)
        p_ctx = pr[..., kh * GRID_W:]
        return (jnp.einsum('bhpqkw,bpkwhd->bpqhd', p_loc, vb)
                + jnp.einsum('bhpqk,bkhd->bpqhd', p_ctx, cv))

    out = lax.map(block, (qb_all, jnp.arange(nb)))
    return out.transpose(1, 0, 2, 3, 4, 5).reshape(B, N, H, Dh)


def conv_module(z, dw, b, g):
    a, gt = jnp.split(z, 2, axis=-1)
    h = a * jax.nn.sigmoid(gt)
    h = lax.conv_general_dilated(h, dw[:, None, :], window_strides=(1,),
                                 padding=[(CONV_WIDTH // 2, CONV_WIDTH // 2)],
                                 dimension_numbers=('NWC', 'WIO', 'NWC'),
                                 feature_group_count=D_CONV) + b
    return jax.nn.silu(rmsnorm(h, g))


def pool_mixer(z, pool_w, pool_scale):
    B, N, C = z.shape
    zf = z.astype(jnp.float32)
    cs = jnp.concatenate([jnp.zeros((B, 1, C), jnp.float32), jnp.cumsum(zf, axis=1)], axis=1)
    t = jnp.arange(N)
    outs = []
    for gi, w in enumerate(POOL_WINDOWS):
        lo = jnp.clip(t - w // 2, 0, N)
        hi = jnp.clip(t - w // 2 + w, 0, N)
        sl = slice(gi * POOL_GROUP, (gi + 1) * POOL_GROUP)
        mean = (cs[:, hi, sl] - cs[:, lo, sl]) / (hi - lo).astype(jnp.float32)[None, :, None]
        d = (mean - zf[:, :, sl]).astype(z.dtype)
        outs.append(d @ pool_w[gi])
    return jnp.concatenate(outs, axis=-1) * pool_scale


def trunk_layer(x, mod, attend, norm_g, w1, w3, w2, w_in, qk_g, w_br_a, conv_dw, conv_b, conv_g,
                w_br_b, pool_w, pool_scale, w_br_c, w_out):
    sh1, sc1, g1, sh2, sc2, g2, sh3, sc3, g3 = jnp.split(mod[:, None, :], N_MOD, axis=-1)
    B, N, _ = x.shape
    h = rmsnorm(x, norm_g[0]) * (1.0 + sc1) + sh1
    x = x + 0.5 * g1 * swiglu(h, w1[0], w3[0], w2[0])
    u = rmsnorm(x, norm_g[1]) * (1.0 + sc2) + sh2
    q, k, v, conv_in, pool_in, gates = jnp.split(u @ w_in, IN_SPLITS, axis=-1)
    q = rmsnorm(q.reshape(B, N, N_HEADS, HEAD_DIM), qk_g[0])
    k = rmsnorm(k.reshape(B, N, N_HEADS, HEAD_DIM), qk_g[1])
    v = v.reshape(B, N, N_HEADS, HEAD_DIM)
    br_a = attend(q, k, v).reshape(B, N, D_ATTN) @ w_br_a
    br_b = conv_module(conv_in, conv_dw, conv_b, conv_g) @ w_br_b
    br_c = pool_mixer(pool_in, pool_w, pool_scale) @ w_br_c
    ga, gb, gc = jnp.split(jax.nn.sigmoid(gates), 3, axis=-1)
    x = x + g2 * ((ga * br_a + gb * br_b + gc * br_c) @ w_out)
    h = rmsnorm(x, norm_g[2]) * (1.0 + sc3) + sh3
    x = x + 0.5 * g3 * swiglu(h, w1[1], w3[1], w2[1])
    return x, k, v


def setup_inputs(seed: int = 0) -> dict:
    key = jax.random.key(seed)
    ks = jax.random.split(key, 24)

    def nrm(k, shape, scale):
        return jax.random.normal(k, shape, jnp.float32) * scale

    return {
        'x_prompt': nrm(ks[0], (BATCH, SEQ, D_MODEL), 1.0),
        'x_sample': nrm(ks[1], (DEC_BATCH, DEC_SEQ, D_MODEL), 1.0),
        'cache_k': nrm(ks[2], (DEC_BATCH, DEPTH, PAST_LEN, N_HEADS, HEAD_DIM), 1.0),
        'cache_v': nrm(ks[3], (DEC_BATCH, DEPTH, PAST_LEN, N_HEADS, HEAD_DIM), 1.0),
        'c': nrm(ks[4], (DEC_BATCH, D_MODEL), 1.0),
        'c_ctx': nrm(ks[5], (D_MODEL,), 1.0),
        'w_mod': nrm(ks[6], (DEPTH, D_MODEL, N_MOD * D_MODEL), D_MODEL ** -0.5),
        'b_mod': nrm(ks[7], (DEPTH, N_MOD * D_MODEL), 0.01),
        'norm_g': 1.0 + nrm(ks[8], (DEPTH, 3, D_MODEL), 0.02),
        'ffn_w1': nrm(ks[9], (DEPTH, 2, D_MODEL, D_FF), D_MODEL ** -0.5),
        'ffn_w3': nrm(ks[10], (DEPTH, 2, D_MODEL, D_FF), D_MODEL ** -0.5),
        'ffn_w2': nrm(ks[11], (DEPTH, 2, D_FF, D_MODEL), D_FF ** -0.5),
        'w_in': nrm(ks[12], (DEPTH, D_MODEL, IN_WIDTH), D_MODEL ** -0.5),
        'qk_g': 1.0 + nrm(ks[13], (DEPTH, 2, HEAD_DIM), 0.02),
        'rpb': nrm(ks[14], (DEPTH, N_HEADS, 2 * WIN_H - 1, 2 * WIN_W - 1), 0.1),
        'w_br_a': nrm(ks[15], (DEPTH, D_ATTN, D_MODEL), D_ATTN ** -0.5),
        'conv_dw': nrm(ks[16], (DEPTH, CONV_WIDTH, D_CONV), CONV_WIDTH ** -0.5),
        'conv_b': nrm(ks[17], (DEPTH, D_CONV), 0.01),
        'conv_g': 1.0 + nrm(ks[18], (DEPTH, D_CONV), 0.02),
        'w_br_b': nrm(ks[19], (DEPTH, D_CONV, D_MODEL), D_CONV ** -0.5),
        'pool_w': nrm(ks[20], (DEPTH, len(POOL_WINDOWS), POOL_GROUP, POOL_GROUP), POOL_GROUP ** -0.5),
        'pool_scale': 1.0 + nrm(ks[21], (DEPTH, D_POOL), 0.02),
        'w_br_c': nrm(ks[22], (DEPTH, D_POOL, D_MODEL), D_POOL ** -0.5),
        'w_out': nrm(ks[23], (DEPTH, D_MODEL, D_MODEL), D_MODEL ** -0.5),
    }


def reference(x_prompt, x_sample, cache_k, cache_v, c, c_ctx, w_mod, b_mod, norm_g, ffn_w1, ffn_w3,
              ffn_w2, w_in, qk_g, rpb, w_br_a, conv_dw, conv_b, conv_g, w_br_b, pool_w, pool_scale,
              w_br_c, w_out):
    xp = x_prompt
    xs = x_sample
    new_ks = []
    new_vs = []
    for l in range(DEPTH):
        shared = (norm_g[l], ffn_w1[l], ffn_w3[l], ffn_w2[l], w_in[l], qk_g[l], w_br_a[l], conv_dw[l],
                  conv_b[l], conv_g[l], w_br_b[l], pool_w[l], pool_scale[l], w_br_c[l], w_out[l])
        mod_ctx = jax.nn.silu(c_ctx)[None, :] @ w_mod[l] + b_mod[l]
        mod_lat = jax.nn.silu(c) @ w_mod[l] + b_mod[l]
        xp, k_l, v_l = trunk_layer(xp, mod_ctx, context_attention, *shared)
        new_ks.append(k_l)
        new_vs.append(v_l)
        attend = functools.partial(latent_attention, ck=cache_k[:, l], cv=cache_v[:, l], rpb=rpb[l])
        xs, _, _ = trunk_layer(xs, mod_lat, attend, *shared)
    new_k = jnp.stack(new_ks, axis=1)
    new_v = jnp.stack(new_vs, axis=1)
    return (xp, xs, new_k, new_v)
```

```python
import numpy as np
from contextlib import ExitStack
import concourse.bass as bass
import concourse.mybir as mybir
from concourse.bass_utils import run_bass_kernel_spmd

F32 = mybir.dt.float32
BF16 = mybir.dt.bfloat16
AF = mybir.ActivationFunctionType
ALU = mybir.AluOpType

D = 2048
NCH = 16
T = 1024
NTT = 2
TT = 512
L = 4
DFF = 5504
NFC = 43
NST = 22
H = 16
EPS = 1e-6
NEG = -30000.0
RING = 20480
ARENA = 43008
N_DMA_SEMS = 8

DEBUG = {"NL": L, "groups": ("S", "P")}


class Res:
    __slots__ = ("name", "w", "rs")

    def __init__(self, name=""):
        self.name = name
        self.w = None
        self.rs = {}


class _Eng:
    def __init__(self, name):
        self.name = name
        self.items = []
        self.seq = 0
        self.clock = {}
        self.sems = []
        self.last = None


class _Rec:
    def __init__(self):
        self.call = None

    def __getattr__(self, name):
        def f(*a, **k):
            assert self.call is None
            self.call = (name, a, k)
            return self
        return f


def _record(fn):
    rec = _Rec()
    fn(rec)
    assert rec.call is not None
    return rec.call


class Prog:
    EPOCH = 30000

    def __init__(self, nc, es):
        self.nc = nc
        self.es = es
        self.eng = {n: _Eng(n) for n in ("pe", "act", "dve", "pool", "sp")}
        self.dsem = {}
        self.dval = {}
        self.dlast = {}
        self.drr = {"pool": 0, "sp": 0}
        for q in ("pool", "sp"):
            for i in range(N_DMA_SEMS):
                k = (q, i)
                self.dsem[k] = es.enter_context(nc.semaphore("d_%s_%d" % (q, i)))
                self.dval[k] = 0
                self.dlast[k] = None
        self.out_events = []

    def _sem_for(self, e, seq):
        idx = (seq - 1) // self.EPOCH
        while len(e.sems) <= idx:
            e.sems.append(self.es.enter_context(self.nc.semaphore("e_%s_%d" % (e.name, len(e.sems)))))
        return e.sems[idx], seq - idx * self.EPOCH

    @staticmethod
    def _deps(reads, writes):
        evs = []
        for r in reads:
            if r.w is not None:
                evs.append(r.w)
        for w in writes:
            if w.w is not None:
                evs.append(w.w)
            evs.extend(w.rs.values())
        return evs

    def _waits(self, X, evs):
        e = self.eng[X]
        ck = e.clock
        for ev in evs:
            key, val = ev[0], ev[1]
            if key == X:
                continue
            if ck.get(key, 0) >= val:
                continue
            e.items.append(("w", ev[3], ev[4]))
            for k, v in ev[2].items():
                if ck.get(k, 0) < v:
                    ck[k] = v
            if ck.get(key, 0) < val:
                ck[key] = val

    def op(self, X, fn, r=(), w=()):
        e = self.eng[X]
        self._waits(X, self._deps(r, w))
        e.seq += 1
        sem, sv = self._sem_for(e, e.seq)
        e.items.append(("i", _record(fn), sem))
        clock = dict(e.clock)
        clock[X] = e.seq
        ev = (X, e.seq, clock, sem, sv)
        e.last = ev
        for rr in r:
            rr.rs[X] = ev
        for ww in w:
            ww.w = ev
            ww.rs = {}
        return ev

    def dma(self, Q, fn, r=(), w=(), extra=()):
        e = self.eng[Q]
        i = self.drr[Q]
        self.drr[Q] = (i + 1) % N_DMA_SEMS
        k = (Q, i)
        evs = self._deps(r, w) + [x for x in extra if x is not None]
        if self.dlast[k] is not None:
            evs.append(self.dlast[k])
        self._waits(Q, evs)
        self.dval[k] += 16
        val = self.dval[k]
        sem = self.dsem[k]
        e.items.append(("d", _record(fn), sem))
        key = ("d",) + k
        clock = dict(e.clock)
        ev = (key, val, clock, sem, val)
        self.dlast[k] = ev
        for rr in r:
            rr.rs[key] = ev
        for ww in w:
            ww.w = ev
            ww.rs = {}
        return ev

    def barrier(self, extra_events=()):
        lasts = [self.eng[n].last for n in ("pe", "act", "dve") if self.eng[n].last is not None]
        for X in ("pe", "act", "dve"):
            self._waits(X, list(lasts) + list(extra_events))

    def replay(self, X, h):
        for it in self.eng[X].items:
            if it[0] == "w":
                h.wait_ge(it[1], it[2])
            elif it[0] == "i":
                getattr(h, it[1][0])(*it[1][1], **it[1][2]).then_inc(it[2], 1)
            else:
                getattr(h, it[1][0])(*it[1][1], **it[1][2]).then_inc(it[2], 16)


def _row_start(r):
    return min(max(r - 4, 0), 8)


def sample_attn_plan():
    plan = []
    for t in range(8):
        rows = (2 * t, 2 * t + 1)
        need = set()
        for r in rows:
            s = _row_start(r)
            need.update(range(s, s + 8))
        kts = sorted({kr // 2 for kr in need}, reverse=True)
        ents = []
        for kt in kts:
            W = [[False, False], [False, False]]
            anym = False
            for krp in range(2):
                kr = 2 * kt + krp
                for pr in range(2):
                    s = _row_start(rows[pr])
                    ok = s <= kr <= s + 7
                    W[krp][pr] = not ok
                    anym = anym or (not ok)
            ents.append((kt, (tuple(W[0]), tuple(W[1])) if anym else None))
        plan.append(ents)
    return plan


def mask_patterns():
    pats = []
    for ents in sample_attn_plan():
        for _, p in ents:
            if p is not None and p not in pats:
                pats.append(p)
    return pats


class Stage:
    def __init__(self, loads, compute):
        self.loads = loads
        self.compute = compute
        self.res = {}
        self.views = {}
        self.blocks = {}

    def done(self, key):
        self.blocks[key][3] = True

    def done_all(self):
        for b in self.blocks.values():
            b[3] = True


def build_nc(cfg=None):
    cfg = cfg or DEBUG
    NL = cfg["NL"]
    groups = cfg["groups"]
    LW = NL
    nc = bass.Bass("TRN2", target_bir_lowering=False)
    es = ExitStack()

    def din(name, shape):
        return nc.dram_tensor(name, list(shape), F32, kind="ExternalInput").ap()

    def dout(name, shape):
        return nc.dram_tensor(name, list(shape), F32, kind="ExternalOutput").ap()

    npat = len(mask_patterns())
    d_xs = din("xs", [NCH, 128, T])
    d_xp = din("xp", [NCH, 128, T])
    d_cvec = din("cvec", [128, NCH, 2])
    d_wmod = din("wmod", [LW, 72, 128, 16, 256])
    d_bmod = din("bmod", [128, LW, 144, 2])
    d_ng = din("ng", [128, LW, 3, NCH])
    d_w13 = din("w13", [LW, 2, NST, 128, 2, 16, 256])
    d_w2 = din("w2", [LW, 2, NST, 128, 2, D])
    d_wqkv = din("wqkv", [LW, 8, 128, 16, 3, 128])
    d_wconv = din("wconv", [LW, 4, 128, 16, 2, 128])
    d_wpool = din("wpool", [LW, 4, 128, 16, 128])
    d_wgate = din("wgate", [LW, 16, 128, 16, 3, 128])
    d_wbra = din("wbra", [LW, 16, 128, 8, 128])
    d_wbrb = din("wbrb", [LW, 16, 128, 4, 128])
    d_wbrc = din("wbrc", [LW, 16, 128, 4, 128])
    d_wout = din("wout", [LW, 16, 128, D])
    d_poolw = din("poolw", [LW, 4, 128, 128])
    d_qkg = din("qkg", [128, LW, 2])
    d_cdw = din("cdw", [128, LW, 4, 31])
    d_cb = din("cb", [128, LW, 4])
    d_cg = din("cg", [128, LW, 4])
    d_psc = din("psc", [128, LW, 4])
    d_bias = din("biasT", [LW, 8, 128, 2, 16, 64])
    d_ckT = din("ckT", [LW, 8, 128, 512])
    d_cv = din("cvp", [LW, 8, 128, 4, 128])
    d_maskl = din("maskl", [2, npat, 128])
    d_ind = din("ind", [2, 128])
    d_cnt = din("cnt", [128, 4, 4, 16])
    d_invS = din("invS", [128, 4, T])
    d_invP = din("invP", [128, 4, T])
    d_c32 = din("c32", [128, 256])

    o_ys = dout("ys", [NCH, 128, T])
    o_yp = dout("yp", [NCH, 128, T])
    o_nk = dout("nk", [LW, 8, 128, T])
    o_nv = dout("nv", [LW, 8, 128, 8, 128])

    def sb(name, shape, dt):
        return es.enter_context(nc.sbuf_tensor("s_" + name, list(shape), dt))

    x = sb("x", [128, NCH, T], F32)
    u = sb("u", [128, NCH, T], BF16)
    ring = sb("ring", [128, RING], BF16)
    ao = sb("ao", [128, 8, T], BF16)
    arena = sb("arena", [128, ARENA // 4], F32)
    mods = sb("mods", [128, LW, 144, 2], F32)
    ng = sb("ng", [128, LW, 3, NCH], F32)
    qkg = sb("qkg", [128, LW, 2], F32)
    cdw = sb("cdw", [128, LW, 4, 31], F32)
    cb = sb("cb", [128, LW, 4], F32)
    cg = sb("cg", [128, LW, 4], F32)
    psc = sb("psc", [128, LW, 4], F32)
    c32 = sb("c32", [128, 256], F32)
    onesb = sb("onesb", [128, 128], BF16)
    maskl = sb("maskl", [2, npat, 128], BF16)
    ind = sb("ind", [2, 128], BF16)
    cnt = sb("cnt", [128, 4, 4, 16], F32)
    cvec = sb("cvec", [128, NCH, 2], F32)
    sT = sb("sT", [128, NCH, 2], BF16)
    Av = sb("Av", [128, NCH], F32)
    Gv = sb("Gv", [128, NCH], F32)
    qg8 = sb("qg8", [128, LW], F32)
    ps = es.enter_context(nc.psum_tensor("ps", [128, 8, 512], F32))

    P = Prog(nc, es)
    bank = [Res("bank%d" % i) for i in range(8)]
    R_x = [[Res("x%d_%d" % (n, tt)) for tt in range(NTT)] for n in range(NCH)]
    R_u = [Res("u%d" % tt) for tt in range(NTT)]
    R_ao = [Res("ao%d" % tt) for tt in range(NTT)]
    R_const = Res("const")
    R_mods = Res("mods")
    R_AG = Res("AG")

    def aview(off, nbytes, dt, shape=None):
        assert off % 4 == 0 and nbytes % 4 == 0 and off + nbytes <= ARENA, (off, nbytes)
        v = arena[:, off // 4:(off + nbytes) // 4]
        if dt == BF16:
            v = v.bitcast(BF16)
        return v

    class ArenaAlloc:
        def __init__(self, base=0):
            self.off = base

        def get(self, nbytes, dt):
            v = aview(self.off, nbytes, dt)
            self.off += nbytes
            return v

    ring_state = {"ptr": 0, "live": []}

    def ring_try_alloc(st, size):
        ptr = st["ptr"]
        if ptr + size > RING:
            ptr = 0
        lo, hi = ptr, ptr + size
        over = [b for b in st["live"] if not (b[0] + b[1] <= lo or b[0] >= hi)]
        for b in over:
            if not b[3]:
                return None
        for b in over:
            st["live"].remove(b)
        st["ptr"] = hi
        return lo, over

    def stage_try_load(stg):
        if not stg.loads:
            return True
        trial = {"ptr": ring_state["ptr"], "live": list(ring_state["live"])}
        for (n, fn, key) in stg.loads:
            got = ring_try_alloc(trial, n)
            if got is None:
                return False
            trial["live"].append([got[0], n, None, False])
        for (n, fn, key) in stg.loads:
            off, over = ring_try_alloc(ring_state, n)
            old = [b[2] for b in over]
            r = Res(key)
            blk = [off, n, r, False]
            ring_state["live"].append(blk)
            stg.blocks[key] = blk
            view = ring[:, off:off + n]
            stg.res[key] = r
            stg.views[key] = view
            pairs = fn(view)
            assert len(pairs) == 1
            oa, ia = pairs[0]
            P.dma("pool", (lambda h, oa=oa, ia=ia: h.dma_start(out=oa, in_=ia)), w=[r] + old)
        return True

    def run_stages(stages, ahead=8):
        n = len(stages)
        li = 0
        ci = 0
        while ci < n:
            while li < n and li - ci < ahead:
                if not stage_try_load(stages[li]):
                    break
                li += 1
            assert li > ci, "ring too small for stage"
            stages[ci].compute(stages[ci])
            stages[ci].done_all()
            ci += 1

    def ld(n, key, out_fn):
        return (n, out_fn, key)

    def sp_load(dst, src, res):
        P.dma("sp", (lambda h: h.dma_start(out=dst, in_=src)), w=[res])

    small = [(mods[:], d_bmod), (ng[:], d_ng), (qkg[:], d_qkg), (cdw[:], d_cdw), (cb[:], d_cb),
             (cg[:], d_cg), (psc[:], d_psc), (c32[:], d_c32), (cnt[:], d_cnt), (cvec[:], d_cvec)]
    R_small = [Res("small%d" % i) for i in range(len(small))]
    for (dst, src), r in zip(small, R_small):
        sp_load(dst, src, r)
    R_m2 = [Res("maskl"), Res("ind")]
    P.dma("pool", lambda h: h.dma_start(out=maskl[:], in_=d_maskl), w=[R_m2[0]])
    P.dma("pool", lambda h: h.dma_start(out=ind[:], in_=d_ind), w=[R_m2[1]])
    ones32 = c32[:, 0:128]
    bones32 = c32[:, 128:256]
    P.op("dve", lambda h: h.memset(onesb[:], 1.0), w=[R_const])
    P.op("act", lambda h: h.activation(out=sT[:], in_=cvec[:], func=AF.Silu), r=[R_small[9]], w=[R_const])
    P.op("dve", lambda h: h.tensor_scalar(out=qg8[:], in0=qkg[:, :, 0], scalar1=0.125, scalar2=None, op0=ALU.mult),
         r=[R_small[2]], w=[R_const])
    P.barrier([r.w for r in R_small] + [r.w for r in R_m2])

    def mods_stages():
        per_layer = []
        for l in range(NL):
            stages = []
            per_layer.append(stages)
            for s in range(72):
                def comp(stg, l=l, s=s):
                    wv = stg.views["w"].rearrange("p (k n) -> p k n", k=16)
                    b = bank[s % 2]
                    pv = ps[:, s % 2, 0:4]
                    for j in range(2):
                        for kc in range(16):
                            P.op("pe", (lambda h, j=j, kc=kc: h.matmul(pv[:, 2 * j:2 * j + 2], lhsT=wv[:, kc, j * 128:(j + 1) * 128],
                                                                         rhs=sT[:, kc, :], start=(kc == 0), stop=(kc == 15))),
                                 r=[stg.res["w"], R_const], w=[b])
                    mv = mods[:, l, 2 * s:2 * s + 2, :]
                    P.op("dve", (lambda h: h.tensor_tensor(out=mv, in0=pv.rearrange("p (a b) -> p a b", a=2), in1=mv, op=ALU.add)),
                         r=[b], w=[R_mods])
                stages.append(Stage([ld(4096, "w", (lambda view, l=l, s=s: [(view.rearrange("p (k n) -> p k n", k=16), d_wmod[l, s])]))], comp))
        return per_layer

    def group_pass(gname, mods_extra=None):
        gi = 0 if gname == "S" else 1
        is_s = gname == "S"
        d_x = d_xs if is_s else d_xp
        d_y = o_ys if is_s else o_yp
        plan = sample_attn_plan()
        pats = mask_patterns()
        arena_reads = []

        for n in range(NCH):
            for tt in range(NTT):
                P.dma("sp", (lambda h, n=n, tt=tt: h.dma_start(out=x[:, n, tt * TT:(tt + 1) * TT], in_=d_x[n, :, tt * TT:(tt + 1) * TT])),
                      w=[R_x[n][tt]])

        def mod_view(l, sub, which, c=None):
            base = (3 * sub + which) * 16
            if c is None:
                return mods[:, l, base:base + 16, gi]
            return mods[:, l, base + c, gi:gi + 1]

        _al = ArenaAlloc(0)
        n_sq = [_al.get(2048, F32) for _ in range(2)]
        n_tn = [_al.get(2048, F32) for _ in range(2)]
        n_lnv = _al.get(2048, F32)
        n_rstd = [_al.get(2048, F32) for _ in range(2)]
        nr_sq = [Res(), Res()]
        nr_tn = [Res(), Res()]
        nr_ln = Res()
        nr_rstd = [Res(), Res()]
        def norm_stage(l, sub):
            def comp(stg):
                sq, tn, lnv, rstd = n_sq, n_tn, n_lnv, n_rstd
                r_sq, r_tn, r_ln, r_rstd = nr_sq, nr_tn, nr_ln, nr_rstd
                P.op("dve", (lambda h: h.scalar_tensor_tensor(out=Av[:], in0=mod_view(l, sub, 1), scalar=1.0, in1=ng[:, l, sub, :],
                                                                op0=ALU.add, op1=ALU.mult)), r=[R_mods], w=[R_AG])
                gsc = 1.0 if sub == 1 else 0.5
                P.op("dve", (lambda h: h.tensor_scalar(out=Gv[:], in0=mod_view(l, sub, 2), scalar1=gsc, scalar2=None, op0=ALU.mult)),
                     r=[R_mods], w=[R_AG])
                for tt in range(NTT):
                    sl = slice(tt * TT, (tt + 1) * TT)
                    bk = bank[7]
                    for c in range(NCH):
                        i = c % 2
                        P.op("act", (lambda h, c=c, i=i: h.activation(out=sq[i], in_=x[:, c, sl], func=AF.Square)),
                             r=[R_x[c][tt]], w=[r_sq[i]])
                        P.op("pe", (lambda h, c=c, i=i: h.matmul(ps[:, 7, :], lhsT=ones32, rhs=sq[i], start=(c == 0), stop=(c == NCH - 1))),
                             r=[r_sq[i], R_const], w=[bk])
                    P.op("act", (lambda h: h.activation(out=lnv, in_=ps[:, 7, :], func=AF.Ln, scale=1.0 / D, bias=EPS)), r=[bk], w=[r_ln])
                    P.op("act", (lambda h, tt=tt: h.activation(out=rstd[tt], in_=lnv, func=AF.Exp, scale=-0.5)), r=[r_ln], w=[r_rstd[tt]])
                    for c in range(NCH):
                        i = c % 2
                        P.op("dve", (lambda h, c=c, i=i, tt=tt: h.scalar_tensor_tensor(out=tn[i], in0=x[:, c, sl], scalar=Av[:, c:c + 1], in1=rstd[tt],
                                                                                      op0=ALU.mult, op1=ALU.mult)),
                             r=[R_x[c][tt], R_AG, r_rstd[tt]], w=[r_tn[i]])
                        P.op("act", (lambda h, c=c, i=i: h.activation(out=u[:, c, sl], in_=tn[i], func=AF.Identity,
                                                                      bias=mod_view(l, sub, 0, c), scale=1.0)),
                             r=[r_tn[i], R_mods], w=[R_u[tt]])
            return Stage([], comp)

        ffn_base = 14336
        g_tiles = [aview(ffn_base + i * 4096, 4096, BF16).rearrange("p (j t) -> p j t", j=2) for i in range(2)]
        sa_t = [aview(ffn_base + 8192 + i * 2048, 2048, F32) for i in range(2)]
        r_g = [[Res(), Res()] for _ in range(2)]
        r_sa = [Res(), Res()]

        def ffn_stages(l, f):
            stages = []
            for s in range(NST):
                J = 2 if s < NST - 1 else 1

                def comp_up(stg, s=s, J=J):
                    w13 = stg.views["w13"].rearrange("p (a k n) -> p a k n", a=2, k=16)
                    gt = g_tiles[s % 2]
                    rg = r_g[s % 2]
                    for tt in range(NTT):
                        sl = slice(tt * TT, (tt + 1) * TT)
                        for jj in range(J):
                            ba = jj % 2
                            bb = 2 + jj % 2
                            for kc in range(16):
                                P.op("pe", (lambda h, kc=kc, jj=jj, ba=ba: h.matmul(ps[:, ba, :], lhsT=w13[:, 0, kc, jj * 128:(jj + 1) * 128], rhs=u[:, kc, sl],
                                                                                   start=(kc == 0), stop=(kc == 15))),
                                     r=[stg.res["w13"], R_u[tt]], w=[bank[ba]])
                            for kc in range(16):
                                P.op("pe", (lambda h, kc=kc, jj=jj, bb=bb: h.matmul(ps[:, bb, :], lhsT=w13[:, 1, kc, jj * 128:(jj + 1) * 128], rhs=u[:, kc, sl],
                                                                                   start=(kc == 0), stop=(kc == 15))),
                                     r=[stg.res["w13"], R_u[tt]], w=[bank[bb]])
                            si = jj % 2
                            P.op("act", (lambda h, ba=ba, si=si: h.activation(out=sa_t[si], in_=ps[:, ba, :], func=AF.Silu)),
                                 r=[bank[ba]], w=[r_sa[si]])
                            P.op("dve", (lambda h, bb=bb, si=si, jj=jj: h.tensor_tensor(out=gt[:, jj, sl], in0=sa_t[si], in1=ps[:, bb, :], op=ALU.mult)),
                                 r=[r_sa[si], bank[bb]], w=[rg[tt]])

                stages.append(Stage([ld(2 * 16 * 256, "w13", (lambda view, s=s: [(view.rearrange("p (a k n) -> p a k n", a=2, k=16), d_w13[l, f, s])]))], comp_up))
                if s % 2 == 1:
                    def comp_dn2(stg, s0=s - 1):
                        parts = []
                        for si, key in ((s0, "w2a"), (s0 + 1, "w2b")):
                            Jx = 2 if si < NST - 1 else 1
                            w2 = stg.views[key].rearrange("p (j n) -> p j n", j=2)
                            for jj in range(Jx):
                                parts.append((w2, jj, g_tiles[si % 2], r_g[si % 2], key))
                        npart = len(parts)
                        for tt in range(NTT):
                            sl = slice(tt * TT, (tt + 1) * TT)
                            for n in range(NCH):
                                bo = 4 + (n % 3)
                                for pi, (w2, jj, gt, rg, key) in enumerate(parts):
                                    P.op("pe", (lambda h, n=n, bo=bo, w2=w2, jj=jj, gt=gt, pi=pi: h.matmul(ps[:, bo, :], lhsT=w2[:, jj, n * 128:(n + 1) * 128], rhs=gt[:, jj, sl],
                                                                                                         start=(pi == 0), stop=(pi == npart - 1))),
                                         r=[stg.res[key], rg[tt]], w=[bank[bo]])
                                P.op("dve", (lambda h, n=n, bo=bo: h.scalar_tensor_tensor(out=x[:, n, sl], in0=ps[:, bo, :], scalar=Gv[:, n:n + 1], in1=x[:, n, sl],
                                                                                         op0=ALU.mult, op1=ALU.add)),
                                     r=[bank[bo], R_AG], w=[R_x[n][tt]])
                    stages.append(Stage([ld(2 * D, "w2a", (lambda view, s=s: [(view.rearrange("p (j n) -> p j n", j=2), d_w2[l, f, s - 1])])),
                                         ld(2 * D, "w2b", (lambda view, s=s: [(view.rearrange("p (j n) -> p j n", j=2), d_w2[l, f, s])]))], comp_dn2))
            return stages

        def attn_stages(l):
            stages = []
            al = ArenaAlloc(0)
            qT = al.get(2048, BF16)
            kT = al.get(2048, BF16)
            vt = al.get(2048, BF16).rearrange("p (t n) -> p t n", t=8)
            Pb = [al.get(2304, BF16) for _ in range(2)]
            Sb = [al.get(2560, F32) for _ in range(2)]
            rden = [al.get(512, F32) for _ in range(2)]
            sq = [al.get(2048, F32) for _ in range(2)]
            lnv = [al.get(2048, F32) for _ in range(2)]
            rst = [al.get(2048, F32) for _ in range(2)]
            kn32 = [al.get(2048, F32) for _ in range(2)]
            v32 = al.get(4096, F32).rearrange("p (t n) -> p t n", t=8)
            r_q = [Res(), Res()]
            r_k = [Res(), Res()]
            r_v = Res()
            r_P = [Res(), Res()]
            r_Sb = [Res(), Res()]
            r_rden = [Res(), Res()]
            r_sq = [Res(), Res()]
            r_ln = [Res(), Res()]
            r_rst = [Res(), Res()]
            r_kn32 = [Res(), Res()]
            r_v32 = Res()
            cnt_i = [0]

            for c in range(8):
                def comp(stg, c=c):
                    wq = stg.views["wqkv"].rearrange("p (k a n) -> p k a n", k=16, a=3)
                    for which in range(2):
                        dstT = qT if which == 0 else kT
                        rdst = r_q if which == 0 else r_k
                        gcol = qg8[:, l:l + 1] if which == 0 else qkg[:, l, 1:2]
                        for tt in range(NTT):
                            sl = slice(tt * TT, (tt + 1) * TT)
                            i = cnt_i[0] % 2
                            cnt_i[0] += 1
                            bq = i
                            bs = 2 + i
                            for kc in range(16):
                                P.op("pe", (lambda h, kc=kc, bq=bq, which=which: h.matmul(ps[:, bq, :], lhsT=wq[:, kc, which, :], rhs=u[:, kc, sl],
                                                                                         start=(kc == 0), stop=(kc == 15))),
                                     r=[stg.res["wqkv"], R_u[tt]], w=[bank[bq]])
                            P.op("act", (lambda h, i=i, bq=bq: h.activation(out=sq[i], in_=ps[:, bq, :], func=AF.Square)), r=[bank[bq]], w=[r_sq[i]])
                            P.op("pe", (lambda h, i=i, bs=bs: h.matmul(ps[:, bs, :], lhsT=bones32, rhs=sq[i], start=True, stop=True)),
                                 r=[r_sq[i], R_const], w=[bank[bs]])
                            P.op("act", (lambda h, i=i, bs=bs: h.activation(out=lnv[i], in_=ps[:, bs, :], func=AF.Ln, scale=1.0 / 64, bias=EPS)),
                                 r=[bank[bs]], w=[r_ln[i]])
                            P.op("act", (lambda h, i=i: h.activation(out=rst[i], in_=lnv[i], func=AF.Exp, scale=-0.5)), r=[r_ln[i]], w=[r_rst[i]])
                            if which == 1 and not is_s:
                                P.op("dve", (lambda h, i=i, bq=bq: h.scalar_tensor_tensor(out=kn32[i], in0=ps[:, bq, :], scalar=gcol, in1=rst[i],
                                                                                         op0=ALU.mult, op1=ALU.mult)),
                                     r=[bank[bq], r_rst[i], R_const], w=[r_kn32[i]])
                                P.op("act", (lambda h, i=i: h.activation(out=dstT[:, sl], in_=kn32[i], func=AF.Copy)), r=[r_kn32[i]], w=[rdst[tt]])
                                ev = P.dma("sp", (lambda h, i=i, tt=tt: h.dma_start(out=o_nk[l, c, :, tt * TT:(tt + 1) * TT], in_=kn32[i])), r=[r_kn32[i]])
                                arena_reads.append(ev)
                                P.out_events.append(ev)
                            else:
                                P.op("dve", (lambda h, i=i, bq=bq: h.scalar_tensor_tensor(out=dstT[:, sl], in0=ps[:, bq, :], scalar=gcol, in1=rst[i],
                                                                                         op0=ALU.mult, op1=ALU.mult)),
                                     r=[bank[bq], r_rst[i], R_const], w=[rdst[tt]])
                    if cfg.get('attpart', 9) < 2:
                        return
                    for t8 in range(8):
                        bv = 4 + (t8 // 4)
                        co = (t8 % 4) * 128
                        for kc in range(16):
                            P.op("pe", (lambda h, kc=kc, t8=t8, bv=bv, co=co: h.matmul(ps[:, bv, co:co + 128], lhsT=u[:, kc, t8 * 128:(t8 + 1) * 128], rhs=wq[:, kc, 2, :],
                                                                                      start=(kc == 0), stop=(kc == 15))),
                                 r=[stg.res["wqkv"], R_u[t8 // 4]], w=[bank[bv]])
                    for hb in range(2):
                        bv = 4 + hb
                        pvv = ps[:, bv, :].rearrange("p (t n) -> p t n", t=4)
                        if cfg.get('vsub', 9) >= 1:
                            P.op("act", (lambda h, hb=hb, pvv=pvv: h.activation(out=vt[:, 4 * hb:4 * hb + 4, :], in_=pvv, func=AF.Copy)), r=[bank[bv]], w=[r_v])
                        if not is_s and cfg.get('vsub', 9) >= 2:
                            P.op("dve", (lambda h, hb=hb: h.tensor_copy(out=v32[:, 4 * hb:4 * hb + 4, :], in_=pvv)), r=[], w=[r_v32, bank[bv]])
                    if not is_s and cfg.get('vsub', 9) >= 3:
                        ev = P.dma("sp", (lambda h: h.dma_start(out=o_nv[l, c], in_=v32[:])), r=[r_v32])
                        arena_reads.append(ev)
                        P.out_events.append(ev)
                    if cfg.get('attpart', 9) < 3:
                        return
                    it = 0
                    for hh in range(2):
                        pb = hh * 64
                        seqs = [0] if is_s else [0, 1, 2, 3]
                        for sq_i in seqs:
                            ntile = 8 if is_s else 2
                            for tq in range(ntile):
                                t = tq if is_s else sq_i * 2 + tq
                                if is_s:
                                    ents = plan[t]
                                else:
                                    ents = [(sq_i * 2 + 1, None), (sq_i * 2, None)]
                                nloc = len(ents)
                                nctx = 4 if is_s else 0
                                par = it % 2
                                it += 1
                                b0 = 4 * par
                                Sl = ps[:, b0:b0 + 2, :].rearrange("p a n -> p (a n)")
                                Sc = ps[:, b0 + 2, :]
                                ob = ps[:, b0 + 3, :]
                                qsl = qT[pb:pb + 64, t * 128:(t + 1) * 128]
                                for i, (kt, pat) in enumerate(ents):
                                    bki = b0 + (i // 4)
                                    P.op("pe", (lambda h, i=i, kt=kt, pat=pat: h.matmul(Sl[:, i * 128:(i + 1) * 128], lhsT=kT[pb:pb + 64, kt * 128:(kt + 1) * 128], rhs=qsl,
                                                                                     start=True, stop=(pat is None))),
                                         r=[r_k[kt // 4], r_q[t // 4]], w=[bank[bki]])
                                    if pat is not None:
                                        pi = pats.index(pat)
                                        P.op("pe", (lambda h, i=i, pi=pi: h.matmul(Sl[:, i * 128:(i + 1) * 128], lhsT=maskl[:, pi, :], rhs=ind[:, :],
                                                                                 start=False, stop=True)),
                                             r=[R_const], w=[bank[bki]])
                                for j in range(nctx):
                                    ck = stg.views["ck"]
                                    P.op("pe", (lambda h, j=j, ck=ck: h.matmul(Sc[:, j * 128:(j + 1) * 128], lhsT=ck[pb:pb + 64, j * 128:(j + 1) * 128], rhs=qsl,
                                                                            start=True, stop=True)),
                                         r=[stg.res["ck"], r_q[t // 4]], w=[bank[b0 + 2]])
                                Pt = Pb[par]
                                sbanks = [bank[b0]] + ([bank[b0 + 1]] if nloc > 4 else [])
                                if is_s:
                                    bt = stg.views["bias"].rearrange("p (a d c) -> p a d c", a=2, d=16)
                                    d0 = 7 - 2 * (ents[0][0] - t)
                                    bsl = bt[:, hh, d0:d0 + 2 * nloc, :].rearrange("p d c -> p (d c)")
                                    P.op("dve", (lambda h, par=par, nloc=nloc, bsl=bsl, Sl=Sl: h.tensor_tensor(out=Sb[par][:, 0:nloc * 128], in0=Sl[:, 0:nloc * 128], in1=bsl, op=ALU.add)),
                                         r=sbanks + [stg.res["bias"]], w=[r_Sb[par]])
                                    P.op("act", (lambda h, par=par, nloc=nloc, Pt=Pt: h.activation(out=Pt[:, 0:nloc * 128], in_=Sb[par][:, 0:nloc * 128], func=AF.Exp)),
                                         r=[r_Sb[par]], w=[r_P[par]])
                                    P.op("act", (lambda h, Pt=Pt, Sc=Sc: h.activation(out=Pt[:, 640:1152], in_=Sc, func=AF.Exp)),
                                         r=[bank[b0 + 2]], w=[r_P[par]])
                                else:
                                    P.op("act", (lambda h, nloc=nloc, Pt=Pt, Sl=Sl: h.activation(out=Pt[:, 0:nloc * 128], in_=Sl[:, 0:nloc * 128], func=AF.Exp)),
                                         r=sbanks, w=[r_P[par]])
                                tiles = [(vt[:, kt, :], Pt[:, i * 128:(i + 1) * 128], r_v) for i, (kt, _) in enumerate(ents)]
                                if is_s:
                                    cvv = stg.views["cv"].rearrange("p (j n) -> p j n", j=4)
                                    tiles += [(cvv[:, j, :], Pt[:, 640 + j * 128:640 + (j + 1) * 128], stg.res["cv"]) for j in range(4)]
                                nt = len(tiles)
                                for i, (va, pa, rv) in enumerate(tiles):
                                    P.op("pe", (lambda h, i=i, va=va, pa=pa, ob=ob, nt=nt: h.matmul(ob[:, 0:128], lhsT=va, rhs=pa, start=(i == 0), stop=(i == nt - 1))),
                                         r=[rv, r_P[par]], w=[bank[b0 + 3]])
                                for i, (va, pa, rv) in enumerate(tiles):
                                    P.op("pe", (lambda h, i=i, pa=pa, ob=ob, nt=nt: h.matmul(ob[:, 128:256], lhsT=onesb[:, :], rhs=pa, start=(i == 0), stop=(i == nt - 1))),
                                         r=[r_P[par], R_const], w=[bank[b0 + 3]])
                                P.op("dve", (lambda h, par=par, ob=ob: h.reciprocal(out=rden[par][pb:pb + 64, :], in_=ob[pb:pb + 64, 128:256])),
                                     r=[bank[b0 + 3]], w=[r_rden[par]])
                                P.op("dve", (lambda h, par=par, ob=ob, t=t: h.tensor_tensor(out=ao[pb:pb + 64, c, t * 128:(t + 1) * 128], in0=ob[pb:pb + 64, 0:128],
                                                                                         in1=rden[par][pb:pb + 64, :], op=ALU.mult)),
                                     r=[bank[b0 + 3], r_rden[par]], w=[R_ao[t // 4]])
                loads = [ld(16 * 3 * 128, "wqkv", (lambda view, c=c: [(view.rearrange("p (k a n) -> p k a n", k=16, a=3), d_wqkv[l, c])]))]
                if is_s:
                    loads.append(ld(2 * 16 * 64, "bias", (lambda view, c=c: [(view.rearrange("p (a d c) -> p a d c", a=2, d=16), d_bias[l, c])])))
                    loads.append(ld(512, "ck", (lambda view, c=c: [(view, d_ckT[l, c])])))
                    loads.append(ld(512, "cv", (lambda view, c=c: [(view.rearrange("p (j n) -> p j n", j=4), d_cv[l, c])])))
                stages.append(Stage(loads, comp))
            return stages

        CVO_OFF = ARENA - 8192
        PLO_OFF = ARENA - 16384
        cvo = aview(CVO_OFF, 8192, BF16).rearrange("p (c t) -> p c t", c=4)
        plo = aview(PLO_OFF, 8192, BF16).rearrange("p (c t) -> p c t", c=4)
        R_cvo = [Res(), Res()]
        R_plo = [Res(), Res()]
        nseq = 1 if is_s else 4
        Ls = T // nseq

        def conv_stages(l):
            stages = []
            al = ArenaAlloc(0)
            HP = Ls + 30
            hpad = al.get(nseq * HP * 4, F32).rearrange("p (s n) -> p s n", s=nseq)
            c32t = al.get(16384, F32).rearrange("p (c t) -> p c t", c=4)
            sg = [al.get(2048, F32) for _ in range(2)]
            _sq = al.get(2048, F32)
            sq = [_sq, _sq]
            lnv = al.get(2048, F32)
            _rst = al.get(2048, F32)
            rst = [_rst, _rst]
            _yt = al.get(2048, F32)
            yt = [_yt, _yt]
            assert al.off <= CVO_OFF, al.off
            r_hpad = Res()
            r_c32 = [Res() for _ in range(4)]
            r_sg = [Res(), Res()]
            _r = Res()
            r_sq = [_r, _r]
            r_ln = Res()
            _r2 = Res()
            r_rst = [_r2, _r2]
            _r3 = Res()
            r_yt = [_r3, _r3]
            first = [True]

            for c in range(4):
                def comp(stg, c=c):
                    if first[0]:
                        first[0] = False
                        P.op("dve", lambda h: h.memset(hpad, 0.0), w=[r_hpad])
                    wc = stg.views["wc"].rearrange("p (k a n) -> p k a n", k=16, a=2)
                    for tt in range(NTT):
                        sl = slice(tt * TT, (tt + 1) * TT)
                        ba, bg = tt, 2 + tt
                        for which, bk in ((0, ba), (1, bg)):
                            for kc in range(16):
                                P.op("pe", (lambda h, kc=kc, which=which, bk=bk: h.matmul(ps[:, bk, :], lhsT=wc[:, kc, which, :], rhs=u[:, kc, sl],
                                                                                         start=(kc == 0), stop=(kc == 15))),
                                     r=[stg.res["wc"], R_u[tt]], w=[bank[bk]])
                        P.op("act", (lambda h, tt=tt, bg=bg: h.activation(out=sg[tt], in_=ps[:, bg, :], func=AF.Sigmoid)), r=[bank[bg]], w=[r_sg[tt]])
                        if is_s:
                            hv = hpad[:, 0, 15 + tt * TT:15 + (tt + 1) * TT]
                            av = ps[:, ba, :]
                            sv = sg[tt]
                        else:
                            hv = hpad[:, 2 * tt:2 * tt + 2, 15:15 + Ls]
                            av = ps[:, ba, :].rearrange("p (s n) -> p s n", s=2)
                            sv = sg[tt].rearrange("p (s n) -> p s n", s=2)
                        P.op("dve", (lambda h, hv=hv, av=av, sv=sv: h.tensor_tensor(out=hv, in0=av, in1=sv, op=ALU.mult)),
                             r=[bank[ba], r_sg[tt]], w=[r_hpad])
                    acc = c32t[:, c, :].rearrange("p (s n) -> p s n", s=nseq)
                    P.op("dve", (lambda h: h.tensor_scalar(out=acc, in0=hpad[:, :, 0:Ls], scalar1=cdw[:, l, c, 0:1], scalar2=cb[:, l, c:c + 1],
                                                           op0=ALU.mult, op1=ALU.add)), r=[r_hpad, R_const], w=[r_c32[c]])
                    for j in range(1, 31):
                        P.op("dve", (lambda h, j=j: h.scalar_tensor_tensor(out=acc, in0=hpad[:, :, j:j + Ls], scalar=cdw[:, l, c, j:j + 1], in1=acc,
                                                                          op0=ALU.mult, op1=ALU.add)), r=[r_hpad], w=[r_c32[c]])
                    for tt in range(NTT):
                        sl = slice(tt * TT, (tt + 1) * TT)
                        i = (2 * c + tt) % 2
                        P.op("act", (lambda h, i=i: h.activation(out=sq[i], in_=c32t[:, c, sl], func=AF.Square)), r=[r_c32[c]], w=[r_sq[i]])
                        P.op("pe", (lambda h, i=i, tt=tt: h.matmul(ps[:, 4 + tt, :], lhsT=ones32, rhs=sq[i], start=(c == 0), stop=(c == 3))),
                             r=[r_sq[i], R_const], w=[bank[4 + tt]])
                    if c == 3:
                        for tt in range(NTT):
                            sl = slice(tt * TT, (tt + 1) * TT)
                            P.op("act", (lambda h, tt=tt: h.activation(out=lnv, in_=ps[:, 4 + tt, :], func=AF.Ln, scale=1.0 / 512, bias=EPS)),
                                 r=[bank[4 + tt]], w=[r_ln])
                            P.op("act", (lambda h, tt=tt: h.activation(out=rst[tt], in_=lnv, func=AF.Exp, scale=-0.5)), r=[r_ln], w=[r_rst[tt]])
                            for cc in range(4):
                                i = cc % 2
                                P.op("dve", (lambda h, cc=cc, i=i, tt=tt: h.scalar_tensor_tensor(out=yt[i], in0=c32t[:, cc, sl], scalar=cg[:, l, cc:cc + 1], in1=rst[tt],
                                                                                                op0=ALU.mult, op1=ALU.mult)),
                                     r=[r_c32[cc], r_rst[tt]], w=[r_yt[i]])
                                P.op("act", (lambda h, cc=cc, i=i: h.activation(out=cvo[:, cc, sl], in_=yt[i], func=AF.Silu)), r=[r_yt[i]], w=[R_cvo[tt]])
                stages.append(Stage([ld(16 * 2 * 128, "wc", (lambda view, c=c: [(view.rearrange("p (k a n) -> p k a n", k=16, a=2), d_wconv[l, c])]))], comp))
            return stages

        def pool_stages(l):
            stages = []
            al = ArenaAlloc(0)
            ZP = Ls + 16
            zpad = al.get(nseq * ZP * 4, F32).rearrange("p (s n) -> p s n", s=nseq)
            sA = al.get(nseq * ZP * 4, F32).rearrange("p (s n) -> p s n", s=nseq)
            sB = al.get(nseq * ZP * 4, F32).rearrange("p (s n) -> p s n", s=nseq)
            dbf = al.get(2048, BF16)
            ict = al.get(4096, F32)
            r_ict = Res()
            d_inv = d_invS if is_s else d_invP
            assert al.off <= PLO_OFF
            r_z = Res()
            r_sA = Res()
            r_sB = Res()
            r_d = Res()
            r_tb = Res()
            first = [True]
            wins = (2, 4, 8, 16)
            for g in range(4):
                def comp(stg, g=g):
                    if first[0]:
                        first[0] = False
                        P.op("dve", lambda h: h.memset(zpad, 0.0), w=[r_z])
                    wp = stg.views["wp"].rearrange("p (k n) -> p k n", k=16)
                    pw = stg.views["pw"]
                    for tt in range(NTT):
                        sl = slice(tt * TT, (tt + 1) * TT)
                        for kc in range(16):
                            P.op("pe", (lambda h, kc=kc, tt=tt: h.matmul(ps[:, tt, :], lhsT=wp[:, kc, :], rhs=u[:, kc, sl], start=(kc == 0), stop=(kc == 15))),
                                 r=[stg.res["wp"], R_u[tt]], w=[bank[tt]])
                        if is_s:
                            zv = zpad[:, 0, 8 + tt * TT:8 + (tt + 1) * TT]
                            pv = ps[:, tt, :]
                        else:
                            zv = zpad[:, 2 * tt:2 * tt + 2, 8:8 + Ls]
                            pv = ps[:, tt, :].rearrange("p (s n) -> p s n", s=2)
                        P.op("act", (lambda h, zv=zv, pv=pv: h.activation(out=zv, in_=pv, func=AF.Copy)), r=[bank[tt]], w=[r_z])
                    w = wins[g]
                    P.op("dve", lambda h: h.tensor_tensor(out=sA[:, :, 1:ZP], in0=zpad[:, :, 0:ZP - 1], in1=zpad[:, :, 1:ZP], op=ALU.add),
                         r=[r_z], w=[r_sA])
                    cur, rcur, oth, roth = sA, r_sA, sB, r_sB
                    lo, hi = 1, ZP
                    sh = 1
                    for _ in range(g):
                        nlo, nhi = lo + sh, hi - sh
                        P.op("dve", (lambda h, cur=cur, oth=oth, nlo=nlo, nhi=nhi, sh=sh: h.tensor_tensor(out=oth[:, :, nlo:nhi], in0=cur[:, :, nlo - sh:nhi - sh],
                                                                                                      in1=cur[:, :, nlo + sh:nhi + sh], op=ALU.add)),
                             r=[rcur], w=[roth])
                        cur, rcur, oth, roth = oth, roth, cur, rcur
                        lo, hi = nlo, nhi
                        sh *= 2
                    Sv = cur[:, :, 8:8 + Ls]
                    zi = zpad[:, :, 8:8 + Ls]
                    dv = dbf.rearrange("p (s n) -> p s n", s=nseq)
                    P.dma("sp", (lambda h: h.dma_start(out=ict, in_=d_inv[:, g, :])), w=[r_ict],
                          extra=[P.eng[n].last for n in ("pe", "act", "dve")])
                    tmpv = oth[:, :, 8:8 + Ls]
                    P.op("dve", (lambda h: h.tensor_tensor(out=tmpv, in0=Sv, in1=ict.rearrange("p (s n) -> p s n", s=nseq), op=ALU.mult)),
                         r=[rcur, r_ict], w=[roth])
                    P.op("dve", (lambda h: h.tensor_tensor(out=dv, in0=tmpv, in1=zi, op=ALU.subtract)),
                         r=[roth, r_z], w=[r_d])
                    for tt in range(NTT):
                        sl = slice(tt * TT, (tt + 1) * TT)
                        P.op("pe", (lambda h, tt=tt: h.matmul(ps[:, 2 + tt, :], lhsT=pw, rhs=dbf[:, sl], start=True, stop=True)),
                             r=[stg.res["pw"], r_d], w=[bank[2 + tt]])
                        P.op("act", (lambda h, tt=tt: h.activation(out=plo[:, g, sl], in_=ps[:, 2 + tt, :], func=AF.Identity, scale=psc[:, l, g:g + 1], bias=0.0)),
                             r=[bank[2 + tt], R_const], w=[R_plo[tt]])
                loads = [ld(16 * 128, "wp", (lambda view, g=g: [(view.rearrange("p (k n) -> p k n", k=16), d_wpool[l, g])])),
                         ld(128, "pw", (lambda view, g=g: [(view, d_poolw[l, g])]))]
                stages.append(Stage(loads, comp))
            return stages

        def merge_stages(l):
            stages = []
            al = ArenaAlloc(0)
            mg = [al.get(1024, BF16) for _ in range(2)]
            sgt = [al.get(2048, F32) for _ in range(3)]
            mt = [al.get(2048, F32) for _ in range(3)]
            assert al.off <= PLO_OFF
            r_mg = [Res(), Res()]
            r_sgt = [Res() for _ in range(3)]
            r_mt = [Res() for _ in range(3)]
            for n in range(NCH):
                def comp(stg, n=n):
                    wg = stg.views["wg"].rearrange("p (k a n) -> p k a n", k=16, a=3)
                    wba = stg.views["wba"].rearrange("p (k n) -> p k n", k=8)
                    wbb = stg.views["wbb"].rearrange("p (k n) -> p k n", k=4)
                    wbc = stg.views["wbc"].rearrange("p (k n) -> p k n", k=4)
                    wo = stg.views["wo"]
                    brs = [(wba, 8, ao, R_ao, "wba"), (wbb, 4, cvo, R_cvo, "wbb"), (wbc, 4, plo, R_plo, "wbc")]
                    for tt in range(NTT):
                        sl = slice(tt * TT, (tt + 1) * TT)
                        for a in range(3):
                            bg = 2 * a
                            bb = 2 * a + 1
                            for kc in range(16):
                                P.op("pe", (lambda h, kc=kc, a=a, bg=bg: h.matmul(ps[:, bg, :], lhsT=wg[:, kc, a, :], rhs=u[:, kc, sl], start=(kc == 0), stop=(kc == 15))),
                                     r=[stg.res["wg"], R_u[tt]], w=[bank[bg]])
                            wb, nk, src, rsrc, key = brs[a]
                            for kc in range(nk):
                                P.op("pe", (lambda h, kc=kc, wb=wb, src=src, bb=bb, nk=nk: h.matmul(ps[:, bb, :], lhsT=wb[:, kc, :], rhs=src[:, kc, sl],
                                                                                               start=(kc == 0), stop=(kc == nk - 1))),
                                     r=[stg.res[key], rsrc[tt]], w=[bank[bb]])
                            P.op("act", (lambda h, a=a, bg=bg: h.activation(out=sgt[a], in_=ps[:, bg, :], func=AF.Sigmoid)), r=[bank[bg]], w=[r_sgt[a]])
                            P.op("dve", (lambda h, a=a, bb=bb: h.tensor_tensor(out=mt[a], in0=sgt[a], in1=ps[:, bb, :], op=ALU.mult)),
                                 r=[r_sgt[a], bank[bb]], w=[r_mt[a]])
                        P.op("dve", (lambda h: h.tensor_tensor(out=mt[0], in0=mt[0], in1=mt[1], op=ALU.add)), r=[r_mt[1]], w=[r_mt[0]])
                        P.op("dve", (lambda h, tt=tt: h.tensor_tensor(out=mg[tt], in0=mt[0], in1=mt[2], op=ALU.add)), r=[r_mt[0], r_mt[2]], w=[r_mg[tt]])
                        for o in range(NCH):
                            bo = 6 + (o % 2)
                            P.op("pe", (lambda h, o=o, bo=bo, tt=tt: h.matmul(ps[:, bo, :], lhsT=wo[:, o * 128:(o + 1) * 128], rhs=mg[tt], start=True, stop=True)),
                                 r=[stg.res["wo"], r_mg[tt]], w=[bank[bo]])
                            P.op("dve", (lambda h, o=o, bo=bo: h.scalar_tensor_tensor(out=x[:, o, sl], in0=ps[:, bo, :], scalar=Gv[:, o:o + 1], in1=x[:, o, sl],
                                                                                     op0=ALU.mult, op1=ALU.add)),
                                 r=[bank[bo], R_AG], w=[R_x[o][tt]])
                loads = [ld(16 * 3 * 128, "wg", (lambda view, n=n: [(view.rearrange("p (k a n) -> p k a n", k=16, a=3), d_wgate[l, n])])),
                         ld(8 * 128, "wba", (lambda view, n=n: [(view.rearrange("p (k n) -> p k n", k=8), d_wbra[l, n])])),
                         ld(4 * 128, "wbb", (lambda view, n=n: [(view.rearrange("p (k n) -> p k n", k=4), d_wbrb[l, n])])),
                         ld(4 * 128, "wbc", (lambda view, n=n: [(view.rearrange("p (k n) -> p k n", k=4), d_wbrc[l, n])])),
                         ld(D, "wo", (lambda view, n=n: [(view, d_wout[l, n])]))]
                stages.append(Stage(loads, comp))
            return stages

        def bar_stage():
            def comp(stg):
                P.barrier(list(arena_reads))
                del arena_reads[:]
            return Stage([], comp)

        stages = []
        for l in range(NL):
            stages.append(norm_stage(l, 0))
            stages += ffn_stages(l, 0)
            stages.append(norm_stage(l, 1))
            stages.append(bar_stage())
            pend = list(mods_extra[l + 1]) if (mods_extra is not None and l + 1 < NL) else []
            for st in attn_stages(l):
                stages.append(st)
                for _ in range(3):
                    if pend:
                        stages.append(pend.pop(0))
            stages.append(bar_stage())
            stages += conv_stages(l)
            stages.append(bar_stage())
            stages += pool_stages(l)
            stages.append(bar_stage())
            for st in merge_stages(l):
                stages.append(st)
                for _ in range(3):
                    if pend:
                        stages.append(pend.pop(0))
            stages += pend
            stages.append(bar_stage())
            stages.append(norm_stage(l, 2))
            stages += ffn_stages(l, 1)
        run_stages(stages[:cfg.get('nst', 100000)])
        for n in range(NCH):
            for tt in range(NTT):
                ev = P.dma("sp", (lambda h, n=n, tt=tt: h.dma_start(out=d_y[n, :, tt * TT:(tt + 1) * TT], in_=x[:, n, tt * TT:(tt + 1) * TT])),
                           r=[R_x[n][tt]])
                P.out_events.append(ev)

    ms = mods_stages()
    if groups:
        run_stages(ms[0])
        for gi_, g in enumerate(groups):
            group_pass(g, ms if gi_ == 0 else None)
    else:
        for m_ in ms:
            run_stages(m_)
    P._waits("sp", P.out_events)
    with nc.Block() as block:
        @block.tensor
        def _(h):
            P.replay("pe", h)

        @block.scalar
        def _(h):
            P.replay("act", h)

        @block.vector
        def _(h):
            P.replay("dve", h)

        @block.gpsimd
        def _(h):
            P.replay("pool", h)

        @block.sync
        def _(h):
            P.replay("sp", h)
    es.close()
    return nc


def _fm(a):
    t, d = a.shape
    return np.ascontiguousarray(a.T.reshape(d // 128, 128, t))


def _vec_fm(v, nchunks):
    lead = v.shape[:-1]
    r = v.reshape(lead + (nchunks, 128))
    return np.ascontiguousarray(np.moveaxis(r, -1, 0))


def prep_shared(inp):
    f = np.float32
    sh = {}
    L = inp["w_mod"].shape[0]
    w_mod = inp["w_mod"]
    sh["wmod"] = np.ascontiguousarray(w_mod.reshape(L, 16, 128, 72, 256).transpose(0, 3, 2, 1, 4))
    bm = _vec_fm(inp["b_mod"], 144)
    sh["bmod"] = np.ascontiguousarray(np.repeat(bm[..., None], 2, axis=-1))
    sh["ng"] = _vec_fm(inp["norm_g"], 16)
    w1 = inp["ffn_w1"]
    w3 = inp["ffn_w3"]
    w2 = inp["ffn_w2"]
    w13 = np.zeros((L, 2, NST, 128, 2, 16, 256), f)
    PADF = NST * 256
    for a, w in ((0, w1), (1, w3)):
        wp = np.zeros((L, 2, D, PADF), f)
        wp[..., :DFF] = w
        w13[:, :, :, :, a] = wp.reshape(L, 2, 16, 128, NST, 256).transpose(0, 1, 4, 3, 2, 5)
        del wp
    sh["w13"] = w13
    w2p = np.zeros((L, 2, PADF, D), f)
    w2p[:, :, :DFF] = w2
    sh["w2"] = np.ascontiguousarray(w2p.reshape(L, 2, NST, 2, 128, D).transpose(0, 1, 2, 4, 3, 5))
    del w2p
    w_in = inp["w_in"].reshape(L, 16, 128, 10752)
    qkv = w_in[..., 0:3072].reshape(L, 16, 128, 3, 8, 128)
    sh["wqkv"] = np.ascontiguousarray(qkv.transpose(0, 4, 2, 1, 3, 5))
    cv = w_in[..., 3072:4096].reshape(L, 16, 128, 2, 4, 128)
    sh["wconv"] = np.ascontiguousarray(cv.transpose(0, 4, 2, 1, 3, 5))
    pl = w_in[..., 4096:4608].reshape(L, 16, 128, 4, 128)
    sh["wpool"] = np.ascontiguousarray(pl.transpose(0, 3, 2, 1, 4))
    gt = w_in[..., 4608:10752].reshape(L, 16, 128, 3, 16, 128)
    sh["wgate"] = np.ascontiguousarray(gt.transpose(0, 4, 2, 1, 3, 5))
    sh["wbra"] = np.ascontiguousarray(inp["w_br_a"].reshape(L, 8, 128, 16, 128).transpose(0, 3, 2, 1, 4))
    sh["wbrb"] = np.ascontiguousarray(inp["w_br_b"].reshape(L, 4, 128, 16, 128).transpose(0, 3, 2, 1, 4))
    sh["wbrc"] = np.ascontiguousarray(inp["w_br_c"].reshape(L, 4, 128, 16, 128).transpose(0, 3, 2, 1, 4))
    sh["wout"] = np.ascontiguousarray(inp["w_out"].reshape(L, 16, 128, D))
    sh["poolw"] = np.ascontiguousarray(inp["pool_w"])
    qk = inp["qk_g"]
    sh["qkg"] = np.ascontiguousarray(np.concatenate([qk, qk], axis=-1).transpose(2, 0, 1))
    sh["cdw"] = np.ascontiguousarray(inp["conv_dw"].reshape(L, 31, 4, 128).transpose(3, 0, 2, 1))
    sh["cb"] = _vec_fm(inp["conv_b"], 4)
    sh["cg"] = _vec_fm(inp["conv_g"], 4)
    sh["psc"] = _vec_fm(inp["pool_scale"], 4)
    rpb = inp["rpb"]
    kc = np.arange(64)[:, None]
    cc = np.arange(64)[None, :]
    cstart = np.clip(cc - 8, 0, 48)
    colok = (kc >= cstart) & (kc < cstart + 16)
    dcidx = np.clip(kc - cc, -15, 15) + 15
    tab = np.full((L, 16, 2, 64, 16, 64), NEG, f)
    for half in range(2):
        for d in range(16):
            dr = (7 - d) if half == 0 else (8 - d)
            if -7 <= dr <= 7:
                m = rpb[:, :, dr + 7, :][:, :, dcidx]
                tab[:, :, half, :, d, :] = np.where(colok[None, None], m, f(NEG))
    tab = tab.reshape(L, 8, 2, 128, 16, 64).transpose(0, 1, 3, 2, 4, 5)
    sh["biasT"] = np.ascontiguousarray(tab)
    pats = mask_patterns()
    ml = np.zeros((2, len(pats), 128), f)
    for pi, pat in enumerate(pats):
        for prp in range(2):
            for krp in range(2):
                if pat[krp][prp]:
                    ml[prp, pi, krp * 64:(krp + 1) * 64] = -32768.0
    sh["maskl"] = ml
    indm = np.zeros((2, 128), f)
    indm[0, 0:64] = 1.0
    indm[1, 64:128] = 1.0
    sh["ind"] = indm
    c32 = np.zeros((128, 256), f)
    c32[:, 0:128] = 1.0
    c32[0:64, 128:192] = 1.0
    c32[64:128, 192:256] = 1.0
    sh["c32"] = c32
    return sh


def inv_table(Ls):
    tab = np.zeros((128, 4, T), np.float32)
    t = np.arange(Ls)
    for g, w in enumerate((2, 4, 8, 16)):
        lo = np.clip(t - w // 2, 0, Ls)
        hi = np.clip(t - w // 2 + w, 0, Ls)
        inv = (1.0 / (hi - lo)).astype(np.float32)
        tab[:, g, :] = np.tile(inv, T // Ls)[None, :]
    return tab


def cnt_table(Ls):
    tab = np.ones((128, 4, 4, 16), np.float32)
    for g, w in enumerate((2, 4, 8, 16)):
        for t in range(w // 2):
            tab[:, g, :, t] = 1.0 / (t + w // 2)
        nr = w // 2 - 1
        for i in range(nr):
            t = Ls - nr + i
            tab[:, g, :, 8 + i] = 1.0 / (Ls - t + w // 2)
    return tab


_NC_CACHE = {}


def kernel(**inp):
    inp = {k: np.asarray(v) for k, v in inp.items()}
    NL = DEBUG["NL"]
    ncores = DEBUG.get("ncores", 8)
    if NL < L:
        for k in ("w_mod", "b_mod", "norm_g", "ffn_w1", "ffn_w3", "ffn_w2", "w_in", "qk_g", "rpb", "w_br_a", "conv_dw",
                  "conv_b", "conv_g", "w_br_b", "pool_w", "pool_scale", "w_br_c", "w_out"):
            inp[k] = inp[k][:NL]
        inp["cache_k"] = inp["cache_k"][:, :NL]
        inp["cache_v"] = inp["cache_v"][:, :NL]
    sh = prep_shared(inp)
    key = (DEBUG["NL"], tuple(DEBUG["groups"]), DEBUG.get("stop"), DEBUG.get("nmods"), DEBUG.get("nst"), DEBUG.get("attpart"), DEBUG.get("vsub"))
    import time as _t
    _t0 = _t.time()
    if key not in _NC_CACHE:
        _NC_CACHE[key] = build_nc(DEBUG)
    nc = _NC_CACHE[key]
    xp = inp["x_prompt"]
    xs = inp["x_sample"]
    ck = inp["cache_k"]
    cv = inp["cache_v"]
    in_maps = []
    for c in range(ncores):
        b = c // 2
        m = dict(sh)
        m["xs"] = _fm(xs[b])
        m["xp"] = _fm(xp[4 * c:4 * c + 4].reshape(T, D))
        cvec = np.stack([inp["c"][b], inp["c_ctx"]], axis=-1)
        m["cvec"] = np.ascontiguousarray(cvec.reshape(16, 128, 2).transpose(1, 0, 2))
        ckb = inp["cache_k"][b]
        m["ckT"] = np.ascontiguousarray(ckb.reshape(NL, 512, 8, 128).transpose(0, 2, 3, 1))
        cvb = inp["cache_v"][b]
        m["cvp"] = np.ascontiguousarray(cvb.reshape(NL, 4, 128, 8, 128).transpose(0, 3, 2, 1, 4))
        m["cnt"] = np.concatenate([cnt_table(1024)[:, :, 0:0], cnt_table(256)], axis=2) if False else cnt_table(256)
        m["invS"] = inv_table(1024)
        m["invP"] = inv_table(256)
        in_maps.append(m)
    print('host prep s', _t.time() - _t0, flush=True)
    res = run_bass_kernel_spmd(nc, in_maps, core_ids=list(range(ncores)))
    print('run s', _t.time() - _t0, flush=True)
    outs = res.results
    B, S = 32, 256
    yp = np.zeros((B, S, D), np.float32)
    ys = np.zeros((4, 1024, D), np.float32)
    nk = np.zeros((B, L, S, 16, 64), np.float32)
    nv = np.zeros((B, L, S, 16, 64), np.float32)
    for c in range(ncores):
        o = outs[c]
        yp[4 * c:4 * c + 4] = o["yp"].reshape(D, T).T.reshape(4, S, D)
        if c % 2 == 0:
            ys[c // 2] = o["ys"].reshape(D, T).T
        k = o["nk"].transpose(3, 0, 1, 2).reshape(4, S, NL, 16, 64)
        nk[4 * c:4 * c + 4, :NL] = k.transpose(0, 2, 1, 3, 4)
        v = o["nv"].transpose(3, 2, 0, 1, 4).reshape(T, NL, 1024).reshape(4, S, NL, 16, 64)
        nv[4 * c:4 * c + 4, :NL] = v.transpose(0, 2, 1, 3, 4)
    return (yp, ys, nk, nv)
```

```python
import numpy as np
from contextlib import ExitStack
import concourse.bass as bass
import concourse.mybir as mybir
from concourse.bass_utils import run_bass_kernel_spmd

F32 = mybir.dt.float32
BF16 = mybir.dt.bfloat16
AF = mybir.ActivationFunctionType
ALU = mybir.AluOpType

D = 2048
NCH = 16
T = 1024
NTT = 2
TT = 512
L = 4
DFF = 5504
NFC = 43
NST = 22
H = 16
EPS = 1e-6
NEG = -30000.0
RING = 20480
ARENA = 43008
N_DMA_SEMS = 8

DEBUG = {"NL": L, "groups": ("S", "P")}


class Res:
    __slots__ = ("name", "w", "rs")

    def __init__(self, name=""):
        self.name = name
        self.w = None
        self.rs = {}


class _Eng:
    def __init__(self, name):
        self.name = name
        self.items = []
        self.seq = 0
        self.clock = {}
        self.sems = []
        self.last = None


class _Rec:
    def __init__(self):
        self.call = None

    def __getattr__(self, name):
        def f(*a, **k):
            assert self.call is None
            self.call = (name, a, k)
            return self
        return f


def _record(fn):
    rec = _Rec()
    fn(rec)
    assert rec.call is not None
    return rec.call


class Prog:
    EPOCH = 30000

    def __init__(self, nc, es):
        self.nc = nc
        self.es = es
        self.eng = {n: _Eng(n) for n in ("pe", "act", "dve", "pool", "sp")}
        self.dsem = {}
        self.dval = {}
        self.dlast = {}
        self.drr = {"pool": 0, "sp": 0}
        for q in ("pool", "sp"):
            for i in range(N_DMA_SEMS):
                k = (q, i)
                self.dsem[k] = es.enter_context(nc.semaphore("d_%s_%d" % (q, i)))
                self.dval[k] = 0
                self.dlast[k] = None
        self.out_events = []

    def _sem_for(self, e, seq):
        idx = (seq - 1) // self.EPOCH
        while len(e.sems) <= idx:
            e.sems.append(self.es.enter_context(self.nc.semaphore("e_%s_%d" % (e.name, len(e.sems)))))
        return e.sems[idx], seq - idx * self.EPOCH

    @staticmethod
    def _deps(reads, writes):
        evs = []
        for r in reads:
            if r.w is not None:
                evs.append(r.w)
        for w in writes:
            if w.w is not None:
                evs.append(w.w)
            evs.extend(w.rs.values())
        return evs

    def _waits(self, X, evs):
        e = self.eng[X]
        ck = e.clock
        for ev in evs:
            key, val = ev[0], ev[1]
            if key == X:
                continue
            if ck.get(key, 0) >= val:
                continue
            e.items.append(("w", ev[3], ev[4]))
            for k, v in ev[2].items():
                if ck.get(k, 0) < v:
                    ck[k] = v
            if ck.get(key, 0) < val:
                ck[key] = val

    def op(self, X, fn, r=(), w=()):
        e = self.eng[X]
        self._waits(X, self._deps(r, w))
        e.seq += 1
        sem, sv = self._sem_for(e, e.seq)
        e.items.append(("i", _record(fn), sem))
        clock = dict(e.clock)
        clock[X] = e.seq
        ev = (X, e.seq, clock, sem, sv)
        e.last = ev
        for rr in r:
            rr.rs[X] = ev
        for ww in w:
            ww.w = ev
            ww.rs = {}
        return ev

    def dma(self, Q, fn, r=(), w=(), extra=()):
        e = self.eng[Q]
        i = self.drr[Q]
        self.drr[Q] = (i + 1) % N_DMA_SEMS
        k = (Q, i)
        evs = self._deps(r, w) + [x for x in extra if x is not None]
        if self.dlast[k] is not None:
            evs.append(self.dlast[k])
        self._waits(Q, evs)
        self.dval[k] += 16
        val = self.dval[k]
        sem = self.dsem[k]
        e.items.append(("d", _record(fn), sem))
        key = ("d",) + k
        clock = dict(e.clock)
        ev = (key, val, clock, sem, val)
        self.dlast[k] = ev
        for rr in r:
            rr.rs[key] = ev
        for ww in w:
            ww.w = ev
            ww.rs = {}
        return ev

    def barrier(self, extra_events=()):
        lasts = [self.eng[n].last for n in ("pe", "act", "dve") if self.eng[n].last is not None]
        for X in ("pe", "act", "dve"):
            self._waits(X, list(lasts) + list(extra_events))

    def replay(self, X, h):
        for it in self.eng[X].items:
            if it[0] == "w":
                h.wait_ge(it[1], it[2])
            elif it[0] == "i":
                getattr(h, it[1][0])(*it[1][1], **it[1][2]).then_inc(it[2], 1)
            else:
                getattr(h, it[1][0])(*it[1][1], **it[1][2]).then_inc(it[2], 16)


def _row_start(r):
    return min(max(r - 4, 0), 8)


def sample_attn_plan():
    plan = []
    for t in range(8):
        rows = (2 * t, 2 * t + 1)
        need = set()
        for r in rows:
            s = _row_start(r)
            need.update(range(s, s + 8))
        kts = sorted({kr // 2 for kr in need}, reverse=True)
        ents = []
        for kt in kts:
            W = [[False, False], [False, False]]
            anym = False
            for krp in range(2):
                kr = 2 * kt + krp
                for pr in range(2):
                    s = _row_start(rows[pr])
                    ok = s <= kr <= s + 7
                    W[krp][pr] = not ok
                    anym = anym or (not ok)
            ents.append((kt, (tuple(W[0]), tuple(W[1])) if anym else None))
        plan.append(ents)
    return plan


def mask_patterns():
    pats = []
    for ents in sample_attn_plan():
        for _, p in ents:
            if p is not None and p not in pats:
                pats.append(p)
    return pats


class Stage:
    def __init__(self, loads, compute):
        self.loads = loads
        self.compute = compute
        self.res = {}
        self.views = {}
        self.blocks = {}

    def done(self, key):
        self.blocks[key][3] = True

    def done_all(self):
        for b in self.blocks.values():
            b[3] = True


def build_nc(cfg=None):
    cfg = cfg or DEBUG
    NL = cfg["NL"]
    groups = cfg["groups"]
    LW = NL
    nc = bass.Bass("TRN2", target_bir_lowering=False)
    es = ExitStack()

    def din(name, shape):
        return nc.dram_tensor(name, list(shape), F32, kind="ExternalInput").ap()

    def dout(name, shape):
        return nc.dram_tensor(name, list(shape), F32, kind="ExternalOutput").ap()

    npat = len(mask_patterns())
    d_xs = din("xs", [NCH, 128, T])
    d_xp = din("xp", [NCH, 128, T])
    d_cvec = din("cvec", [128, NCH, 2])
    d_wmod = din("wmod", [LW, 72, 128, 16, 256])
    d_bmod = din("bmod", [128, LW, 144, 2])
    d_ng = din("ng", [128, LW, 3, NCH])
    d_w13 = din("w13", [LW, 2, NST, 128, 2, 16, 256])
    d_w2 = din("w2", [LW, 2, NST, 128, 2, D])
    d_wqkv = din("wqkv", [LW, 8, 128, 16, 3, 128])
    d_wconv = din("wconv", [LW, 4, 128, 16, 2, 128])
    d_wpool = din("wpool", [LW, 4, 128, 16, 128])
    d_wgate = din("wgate", [LW, 16, 128, 16, 3, 128])
    d_wbra = din("wbra", [LW, 16, 128, 8, 128])
    d_wbrb = din("wbrb", [LW, 16, 128, 4, 128])
    d_wbrc = din("wbrc", [LW, 16, 128, 4, 128])
    d_wout = din("wout", [LW, 16, 128, D])
    d_poolw = din("poolw", [LW, 4, 128, 128])
    d_qkg = din("qkg", [128, LW, 2])
    d_cdw = din("cdw", [128, LW, 4, 31])
    d_cb = din("cb", [128, LW, 4])
    d_cg = din("cg", [128, LW, 4])
    d_psc = din("psc", [128, LW, 4])
    d_bias = din("biasT", [LW, 8, 128, 2, 16, 64])
    d_ckT = din("ckT", [LW, 8, 128, 512])
    d_cv = din("cvp", [LW, 8, 128, 4, 128])
    d_maskl = din("maskl", [2, npat, 128])
    d_ind = din("ind", [2, 128])
    d_cnt = din("cnt", [128, 4, 4, 16])
    d_invS = din("invS", [128, 4, T])
    d_invP = din("invP", [128, 4, T])
    d_c32 = din("c32", [128, 256])

    o_ys = dout("ys", [NCH, 128, T])
    o_yp = dout("yp", [NCH, 128, T])
    o_nk = dout("nk", [LW, 8, 128, T])
    o_nv = dout("nv", [LW, 8, 128, 8, 128])

    def sb(name, shape, dt):
        return es.enter_context(nc.sbuf_tensor("s_" + name, list(shape), dt))

    x = sb("x", [128, NCH, T], F32)
    u = sb("u", [128, NCH, T], BF16)
    ring = sb("ring", [128, RING], BF16)
    ao = sb("ao", [128, 8, T], BF16)
    arena = sb("arena", [128, ARENA // 4], F32)
    mods = sb("mods", [128, LW, 144, 2], F32)
    ng = sb("ng", [128, LW, 3, NCH], F32)
    qkg = sb("qkg", [128, LW, 2], F32)
    cdw = sb("cdw", [128, LW, 4, 31], F32)
    cb = sb("cb", [128, LW, 4], F32)
    cg = sb("cg", [128, LW, 4], F32)
    psc = sb("psc", [128, LW, 4], F32)
    c32 = sb("c32", [128, 256], F32)
    onesb = sb("onesb", [128, 128], BF16)
    maskl = sb("maskl", [2, npat, 128], BF16)
    ind = sb("ind", [2, 128], BF16)
    cnt = sb("cnt", [128, 4, 4, 16], F32)
    cvec = sb("cvec", [128, NCH, 2], F32)
    sT = sb("sT", [128, NCH, 2], BF16)
    Av = sb("Av", [128, NCH], F32)
    Gv = sb("Gv", [128, NCH], F32)
    qg8 = sb("qg8", [128, LW], F32)
    ps = es.enter_context(nc.psum_tensor("ps", [128, 8, 512], F32))

    P = Prog(nc, es)
    bank = [Res("bank%d" % i) for i in range(8)]
    R_x = [[Res("x%d_%d" % (n, tt)) for tt in range(NTT)] for n in range(NCH)]
    R_u = [Res("u%d" % tt) for tt in range(NTT)]
    R_ao = [Res("ao%d" % tt) for tt in range(NTT)]
    R_const = Res("const")
    R_mods = Res("mods")
    R_AG = Res("AG")

    def aview(off, nbytes, dt, shape=None):
        assert off % 4 == 0 and nbytes % 4 == 0 and off + nbytes <= ARENA, (off, nbytes)
        v = arena[:, off // 4:(off + nbytes) // 4]
        if dt == BF16:
            v = v.bitcast(BF16)
        return v

    class ArenaAlloc:
        def __init__(self, base=0):
            self.off = base

        def get(self, nbytes, dt):
            v = aview(self.off, nbytes, dt)
            self.off += nbytes
            return v

    ring_state = {"ptr": 0, "live": []}

    def ring_try_alloc(st, size):
        ptr = st["ptr"]
        if ptr + size > RING:
            ptr = 0
        lo, hi = ptr, ptr + size
        over = [b for b in st["live"] if not (b[0] + b[1] <= lo or b[0] >= hi)]
        for b in over:
            if not b[3]:
                return None
        for b in over:
            st["live"].remove(b)
        st["ptr"] = hi
        return lo, over

    def stage_try_load(stg):
        if not stg.loads:
            return True
        trial = {"ptr": ring_state["ptr"], "live": list(ring_state["live"])}
        for (n, fn, key) in stg.loads:
            got = ring_try_alloc(trial, n)
            if got is None:
                return False
            trial["live"].append([got[0], n, None, False])
        for (n, fn, key) in stg.loads:
            off, over = ring_try_alloc(ring_state, n)
            old = [b[2] for b in over]
            r = Res(key)
            blk = [off, n, r, False]
            ring_state["live"].append(blk)
            stg.blocks[key] = blk
            view = ring[:, off:off + n]
            stg.res[key] = r
            stg.views[key] = view
            pairs = fn(view)
            assert len(pairs) == 1
            oa, ia = pairs[0]
            P.dma("pool", (lambda h, oa=oa, ia=ia: h.dma_start(out=oa, in_=ia)), w=[r] + old)
        return True

    def run_stages(stages, ahead=8):
        n = len(stages)
        li = 0
        ci = 0
        while ci < n:
            while li < n and li - ci < ahead:
                if not stage_try_load(stages[li]):
                    break
                li += 1
            assert li > ci, "ring too small for stage"
            stages[ci].compute(stages[ci])
            stages[ci].done_all()
            ci += 1

    def ld(n, key, out_fn):
        return (n, out_fn, key)

    def sp_load(dst, src, res):
        P.dma("sp", (lambda h: h.dma_start(out=dst, in_=src)), w=[res])

    small = [(mods[:], d_bmod), (ng[:], d_ng), (qkg[:], d_qkg), (cdw[:], d_cdw), (cb[:], d_cb),
             (cg[:], d_cg), (psc[:], d_psc), (c32[:], d_c32), (cnt[:], d_cnt), (cvec[:], d_cvec)]
    R_small = [Res("small%d" % i) for i in range(len(small))]
    for (dst, src), r in zip(small, R_small):
        sp_load(dst, src, r)
    R_m2 = [Res("maskl"), Res("ind")]
    P.dma("pool", lambda h: h.dma_start(out=maskl[:], in_=d_maskl), w=[R_m2[0]])
    P.dma("pool", lambda h: h.dma_start(out=ind[:], in_=d_ind), w=[R_m2[1]])
    ones32 = c32[:, 0:128]
    bones32 = c32[:, 128:256]
    P.op("dve", lambda h: h.memset(onesb[:], 1.0), w=[R_const])
    P.op("act", lambda h: h.activation(out=sT[:], in_=cvec[:], func=AF.Silu), r=[R_small[9]], w=[R_const])
    P.op("dve", lambda h: h.tensor_scalar(out=qg8[:], in0=qkg[:, :, 0], scalar1=0.125, scalar2=None, op0=ALU.mult),
         r=[R_small[2]], w=[R_const])
    P.barrier([r.w for r in R_small] + [r.w for r in R_m2])

    def mods_stages():
        per_layer = []
        for l in range(NL):
            stages = []
            per_layer.append(stages)
            for s in range(72):
                def comp(stg, l=l, s=s):
                    wv = stg.views["w"].rearrange("p (k n) -> p k n", k=16)
                    b = bank[s % 2]
                    pv = ps[:, s % 2, 0:4]
                    for j in range(2):
                        for kc in range(16):
                            P.op("pe", (lambda h, j=j, kc=kc: h.matmul(pv[:, 2 * j:2 * j + 2], lhsT=wv[:, kc, j * 128:(j + 1) * 128],
                                                                         rhs=sT[:, kc, :], start=(kc == 0), stop=(kc == 15))),
                                 r=[stg.res["w"], R_const], w=[b])
                    mv = mods[:, l, 2 * s:2 * s + 2, :]
                    P.op("dve", (lambda h: h.tensor_tensor(out=mv, in0=pv.rearrange("p (a b) -> p a b", a=2), in1=mv, op=ALU.add)),
                         r=[b], w=[R_mods])
                stages.append(Stage([ld(4096, "w", (lambda view, l=l, s=s: [(view.rearrange("p (k n) -> p k n", k=16), d_wmod[l, s])]))], comp))
        return per_layer

    def group_pass(gname, mods_extra=None):
        gi = 0 if gname == "S" else 1
        is_s = gname == "S"
        d_x = d_xs if is_s else d_xp
        d_y = o_ys if is_s else o_yp
        plan = sample_attn_plan()
        pats = mask_patterns()
        arena_reads = []

        for n in range(NCH):
            for tt in range(NTT):
                P.dma("sp", (lambda h, n=n, tt=tt: h.dma_start(out=x[:, n, tt * TT:(tt + 1) * TT], in_=d_x[n, :, tt * TT:(tt + 1) * TT])),
                      w=[R_x[n][tt]])

        def mod_view(l, sub, which, c=None):
            base = (3 * sub + which) * 16
            if c is None:
                return mods[:, l, base:base + 16, gi]
            return mods[:, l, base + c, gi:gi + 1]

        _al = ArenaAlloc(0)
        n_sq = [_al.get(2048, F32) for _ in range(2)]
        n_tn = [_al.get(2048, F32) for _ in range(2)]
        n_lnv = _al.get(2048, F32)
        n_rstd = [_al.get(2048, F32) for _ in range(2)]
        nr_sq = [Res(), Res()]
        nr_tn = [Res(), Res()]
        nr_ln = Res()
        nr_rstd = [Res(), Res()]
        def norm_stage(l, sub):
            def comp(stg):
                sq, tn, lnv, rstd = n_sq, n_tn, n_lnv, n_rstd
                r_sq, r_tn, r_ln, r_rstd = nr_sq, nr_tn, nr_ln, nr_rstd
                P.op("dve", (lambda h: h.scalar_tensor_tensor(out=Av[:], in0=mod_view(l, sub, 1), scalar=1.0, in1=ng[:, l, sub, :],
                                                                op0=ALU.add, op1=ALU.mult)), r=[R_mods], w=[R_AG])
                gsc = 1.0 if sub == 1 else 0.5
                P.op("dve", (lambda h: h.tensor_scalar(out=Gv[:], in0=mod_view(l, sub, 2), scalar1=gsc, scalar2=None, op0=ALU.mult)),
                     r=[R_mods], w=[R_AG])
                for tt in range(NTT):
                    sl = slice(tt * TT, (tt + 1) * TT)
                    bk = bank[7]
                    for c in range(NCH):
                        i = c % 2
                        P.op("act", (lambda h, c=c, i=i: h.activation(out=sq[i], in_=x[:, c, sl], func=AF.Square)),
                             r=[R_x[c][tt]], w=[r_sq[i]])
                        P.op("pe", (lambda h, c=c, i=i: h.matmul(ps[:, 7, :], lhsT=ones32, rhs=sq[i], start=(c == 0), stop=(c == NCH - 1))),
                             r=[r_sq[i], R_const], w=[bk])
                    P.op("act", (lambda h: h.activation(out=lnv, in_=ps[:, 7, :], func=AF.Ln, scale=1.0 / D, bias=EPS)), r=[bk], w=[r_ln])
                    P.op("act", (lambda h, tt=tt: h.activation(out=rstd[tt], in_=lnv, func=AF.Exp, scale=-0.5)), r=[r_ln], w=[r_rstd[tt]])
                    for c in range(NCH):
                        i = c % 2
                        P.op("dve", (lambda h, c=c, i=i, tt=tt: h.scalar_tensor_tensor(out=tn[i], in0=x[:, c, sl], scalar=Av[:, c:c + 1], in1=rstd[tt],
                                                                                      op0=ALU.mult, op1=ALU.mult)),
                             r=[R_x[c][tt], R_AG, r_rstd[tt]], w=[r_tn[i]])
                        P.op("act", (lambda h, c=c, i=i: h.activation(out=u[:, c, sl], in_=tn[i], func=AF.Identity,
                                                                      bias=mod_view(l, sub, 0, c), scale=1.0)),
                             r=[r_tn[i], R_mods], w=[R_u[tt]])
            return Stage([], comp)

        ffn_base = 14336
        g_tiles = [aview(ffn_base + i * 4096, 4096, BF16).rearrange("p (j t) -> p j t", j=2) for i in range(2)]
        sa_t = [aview(ffn_base + 8192 + i * 2048, 2048, F32) for i in range(2)]
        r_g = [[Res(), Res()] for _ in range(2)]
        r_sa = [Res(), Res()]

        def ffn_stages(l, f):
            stages = []
            for s in range(NST):
                J = 2 if s < NST - 1 else 1

                def comp_up(stg, s=s, J=J):
                    w13 = stg.views["w13"].rearrange("p (a k n) -> p a k n", a=2, k=16)
                    gt = g_tiles[s % 2]
                    rg = r_g[s % 2]
                    for tt in range(NTT):
                        sl = slice(tt * TT, (tt + 1) * TT)
                        for jj in range(J):
                            ba = jj % 2
                            bb = 2 + jj % 2
                            for kc in range(16):
                                P.op("pe", (lambda h, kc=kc, jj=jj, ba=ba: h.matmul(ps[:, ba, :], lhsT=w13[:, 0, kc, jj * 128:(jj + 1) * 128], rhs=u[:, kc, sl],
                                                                                   start=(kc == 0), stop=(kc == 15))),
                                     r=[stg.res["w13"], R_u[tt]], w=[bank[ba]])
                            for kc in range(16):
                                P.op("pe", (lambda h, kc=kc, jj=jj, bb=bb: h.matmul(ps[:, bb, :], lhsT=w13[:, 1, kc, jj * 128:(jj + 1) * 128], rhs=u[:, kc, sl],
                                                                                   start=(kc == 0), stop=(kc == 15))),
                                     r=[stg.res["w13"], R_u[tt]], w=[bank[bb]])
                            si = jj % 2
                            P.op("act", (lambda h, ba=ba, si=si: h.activation(out=sa_t[si], in_=ps[:, ba, :], func=AF.Silu)),
                                 r=[bank[ba]], w=[r_sa[si]])
                            P.op("dve", (lambda h, bb=bb, si=si, jj=jj: h.tensor_tensor(out=gt[:, jj, sl], in0=sa_t[si], in1=ps[:, bb, :], op=ALU.mult)),
                                 r=[r_sa[si], bank[bb]], w=[rg[tt]])

                stages.append(Stage([ld(2 * 16 * 256, "w13", (lambda view, s=s: [(view.rearrange("p (a k n) -> p a k n", a=2, k=16), d_w13[l, f, s])]))], comp_up))
                if s % 2 == 1:
                    def comp_dn2(stg, s0=s - 1):
                        parts = []
                        for si, key in ((s0, "w2a"), (s0 + 1, "w2b")):
                            Jx = 2 if si < NST - 1 else 1
                            w2 = stg.views[key].rearrange("p (j n) -> p j n", j=2)
                            for jj in range(Jx):
                                parts.append((w2, jj, g_tiles[si % 2], r_g[si % 2], key))
                        npart = len(parts)
                        for tt in range(NTT):
                            sl = slice(tt * TT, (tt + 1) * TT)
                            for n in range(NCH):
                                bo = 4 + (n % 4)
                                for pi, (w2, jj, gt, rg, key) in enumerate(parts):
                                    P.op("pe", (lambda h, n=n, bo=bo, w2=w2, jj=jj, gt=gt, pi=pi: h.matmul(ps[:, bo, :], lhsT=w2[:, jj, n * 128:(n + 1) * 128], rhs=gt[:, jj, sl],
                                                                                                         start=(pi == 0), stop=(pi == npart - 1))),
                                         r=[stg.res[key], rg[tt]], w=[bank[bo]])
                                P.op("dve", (lambda h, n=n, bo=bo: h.scalar_tensor_tensor(out=x[:, n, sl], in0=ps[:, bo, :], scalar=Gv[:, n:n + 1], in1=x[:, n, sl],
                                                                                         op0=ALU.mult, op1=ALU.add)),
                                     r=[bank[bo], R_AG], w=[R_x[n][tt]])
                    stages.append(Stage([ld(2 * D, "w2a", (lambda view, s=s: [(view.rearrange("p (j n) -> p j n", j=2), d_w2[l, f, s - 1])])),
                                         ld(2 * D, "w2b", (lambda view, s=s: [(view.rearrange("p (j n) -> p j n", j=2), d_w2[l, f, s])]))], comp_dn2))
            return stages

        def attn_stages(l):
            stages = []
            al = ArenaAlloc(0)
            qT = al.get(2048, BF16)
            kT = al.get(2048, BF16)
            vt = al.get(2048, BF16).rearrange("p (t n) -> p t n", t=8)
            Pb = [al.get(2304, BF16) for _ in range(2)]
            Sb = [al.get(2560, F32) for _ in range(2)]
            rden = [al.get(512, F32) for _ in range(2)]
            sq = [al.get(2048, F32) for _ in range(2)]
            lnv = [al.get(2048, F32) for _ in range(2)]
            rst = [al.get(2048, F32) for _ in range(2)]
            kn32 = [al.get(2048, F32) for _ in range(2)]
            v32 = al.get(4096, F32).rearrange("p (t n) -> p t n", t=8)
            r_q = [Res(), Res()]
            r_k = [Res(), Res()]
            r_v = Res()
            r_P = [Res(), Res()]
            r_Sb = [Res(), Res()]
            r_rden = [Res(), Res()]
            r_sq = [Res(), Res()]
            r_ln = [Res(), Res()]
            r_rst = [Res(), Res()]
            r_kn32 = [Res(), Res()]
            r_v32 = Res()
            cnt_i = [0]

            for c in range(8):
                def comp(stg, c=c):
                    wq = stg.views["wqkv"].rearrange("p (k a n) -> p k a n", k=16, a=3)
                    for which in range(2):
                        dstT = qT if which == 0 else kT
                        rdst = r_q if which == 0 else r_k
                        gcol = qg8[:, l:l + 1] if which == 0 else qkg[:, l, 1:2]
                        for tt in range(NTT):
                            sl = slice(tt * TT, (tt + 1) * TT)
                            i = cnt_i[0] % 2
                            cnt_i[0] += 1
                            bq = i
                            bs = 2 + i
                            for kc in range(16):
                                P.op("pe", (lambda h, kc=kc, bq=bq, which=which: h.matmul(ps[:, bq, :], lhsT=wq[:, kc, which, :], rhs=u[:, kc, sl],
                                                                                         start=(kc == 0), stop=(kc == 15))),
                                     r=[stg.res["wqkv"], R_u[tt]], w=[bank[bq]])
                            P.op("act", (lambda h, i=i, bq=bq: h.activation(out=sq[i], in_=ps[:, bq, :], func=AF.Square)), r=[bank[bq]], w=[r_sq[i]])
                            P.op("pe", (lambda h, i=i, bs=bs: h.matmul(ps[:, bs, :], lhsT=bones32, rhs=sq[i], start=True, stop=True)),
                                 r=[r_sq[i], R_const], w=[bank[bs]])
                            P.op("act", (lambda h, i=i, bs=bs: h.activation(out=lnv[i], in_=ps[:, bs, :], func=AF.Ln, scale=1.0 / 64, bias=EPS)),
                                 r=[bank[bs]], w=[r_ln[i]])
                            P.op("act", (lambda h, i=i: h.activation(out=rst[i], in_=lnv[i], func=AF.Exp, scale=-0.5)), r=[r_ln[i]], w=[r_rst[i]])
                            if which == 1 and not is_s:
                                P.op("dve", (lambda h, i=i, bq=bq: h.scalar_tensor_tensor(out=kn32[i], in0=ps[:, bq, :], scalar=gcol, in1=rst[i],
                                                                                         op0=ALU.mult, op1=ALU.mult)),
                                     r=[bank[bq], r_rst[i], R_const], w=[r_kn32[i]])
                                P.op("act", (lambda h, i=i: h.activation(out=dstT[:, sl], in_=kn32[i], func=AF.Copy)), r=[r_kn32[i]], w=[rdst[tt]])
                                ev = P.dma("sp", (lambda h, i=i, tt=tt: h.dma_start(out=o_nk[l, c, :, tt * TT:(tt + 1) * TT], in_=kn32[i])), r=[r_kn32[i]])
                                arena_reads.append(ev)
                                P.out_events.append(ev)
                            else:
                                P.op("dve", (lambda h, i=i, bq=bq: h.scalar_tensor_tensor(out=dstT[:, sl], in0=ps[:, bq, :], scalar=gcol, in1=rst[i],
                                                                                         op0=ALU.mult, op1=ALU.mult)),
                                     r=[bank[bq], r_rst[i], R_const], w=[rdst[tt]])
                    if cfg.get('attpart', 9) < 2:
                        return
                    for t8 in range(8):
                        bv = 4 + (t8 // 4)
                        co = (t8 % 4) * 128
                        for kc in range(16):
                            P.op("pe", (lambda h, kc=kc, t8=t8, bv=bv, co=co: h.matmul(ps[:, bv, co:co + 128], lhsT=u[:, kc, t8 * 128:(t8 + 1) * 128], rhs=wq[:, kc, 2, :],
                                                                                      start=(kc == 0), stop=(kc == 15))),
                                 r=[stg.res["wqkv"], R_u[t8 // 4]], w=[bank[bv]])
                    for hb in range(2):
                        bv = 4 + hb
                        pvv = ps[:, bv, :].rearrange("p (t n) -> p t n", t=4)
                        if cfg.get('vsub', 9) >= 1:
                            P.op("act", (lambda h, hb=hb, pvv=pvv: h.activation(out=vt[:, 4 * hb:4 * hb + 4, :], in_=pvv, func=AF.Copy)), r=[bank[bv]], w=[r_v])
                        if not is_s and cfg.get('vsub', 9) >= 2:
                            P.op("dve", (lambda h, hb=hb: h.tensor_copy(out=v32[:, 4 * hb:4 * hb + 4, :], in_=pvv)), r=[], w=[r_v32, bank[bv]])
                    if not is_s and cfg.get('vsub', 9) >= 3:
                        ev = P.dma("sp", (lambda h: h.dma_start(out=o_nv[l, c], in_=v32[:])), r=[r_v32])
                        arena_reads.append(ev)
                        P.out_events.append(ev)
                    if cfg.get('attpart', 9) < 3:
                        return
                    it = 0
                    for hh in range(2):
                        pb = hh * 64
                        seqs = [0] if is_s else [0, 1, 2, 3]
                        for sq_i in seqs:
                            ntile = 8 if is_s else 2
                            for tq in range(ntile):
                                t = tq if is_s else sq_i * 2 + tq
                                if is_s:
                                    ents = plan[t]
                                else:
                                    ents = [(sq_i * 2 + 1, None), (sq_i * 2, None)]
                                nloc = len(ents)
                                nctx = 4 if is_s else 0
                                par = it % 2
                                it += 1
                                b0 = 4 * par
                                Sl = ps[:, b0:b0 + 2, :].rearrange("p a n -> p (a n)")
                                Sc = ps[:, b0 + 2, :]
                                ob = ps[:, b0 + 3, :]
                                qsl = qT[pb:pb + 64, t * 128:(t + 1) * 128]
                                for i, (kt, pat) in enumerate(ents):
                                    bki = b0 + (i // 4)
                                    P.op("pe", (lambda h, i=i, kt=kt, pat=pat: h.matmul(Sl[:, i * 128:(i + 1) * 128], lhsT=kT[pb:pb + 64, kt * 128:(kt + 1) * 128], rhs=qsl,
                                                                                     start=True, stop=(pat is None))),
                                         r=[r_k[kt // 4], r_q[t // 4]], w=[bank[bki]])
                                    if pat is not None:
                                        pi = pats.index(pat)
                                        P.op("pe", (lambda h, i=i, pi=pi: h.matmul(Sl[:, i * 128:(i + 1) * 128], lhsT=maskl[:, pi, :], rhs=ind[:, :],
                                                                                 start=False, stop=True)),
                                             r=[R_const], w=[bank[bki]])
                                for j in range(nctx):
                                    ck = stg.views["ck"]
                                    P.op("pe", (lambda h, j=j, ck=ck: h.matmul(Sc[:, j * 128:(j + 1) * 128], lhsT=ck[pb:pb + 64, j * 128:(j + 1) * 128], rhs=qsl,
                                                                            start=True, stop=True)),
                                         r=[stg.res["ck"], r_q[t // 4]], w=[bank[b0 + 2]])
                                Pt = Pb[par]
                                sbanks = [bank[b0]] + ([bank[b0 + 1]] if nloc > 4 else [])
                                if is_s:
                                    bt = stg.views["bias"].rearrange("p (a d c) -> p a d c", a=2, d=16)
                                    d0 = 7 - 2 * (ents[0][0] - t)
                                    bsl = bt[:, hh, d0:d0 + 2 * nloc, :].rearrange("p d c -> p (d c)")
                                    P.op("dve", (lambda h, par=par, nloc=nloc, bsl=bsl, Sl=Sl: h.tensor_tensor(out=Sb[par][:, 0:nloc * 128], in0=Sl[:, 0:nloc * 128], in1=bsl, op=ALU.add)),
                                         r=sbanks + [stg.res["bias"]], w=[r_Sb[par]])
                                    P.op("act", (lambda h, par=par, nloc=nloc, Pt=Pt: h.activation(out=Pt[:, 0:nloc * 128], in_=Sb[par][:, 0:nloc * 128], func=AF.Exp)),
                                         r=[r_Sb[par]], w=[r_P[par]])
                                    P.op("act", (lambda h, Pt=Pt, Sc=Sc: h.activation(out=Pt[:, 640:1152], in_=Sc, func=AF.Exp)),
                                         r=[bank[b0 + 2]], w=[r_P[par]])
                                else:
                                    P.op("act", (lambda h, nloc=nloc, Pt=Pt, Sl=Sl: h.activation(out=Pt[:, 0:nloc * 128], in_=Sl[:, 0:nloc * 128], func=AF.Exp)),
                                         r=sbanks, w=[r_P[par]])
                                tiles = [(vt[:, kt, :], Pt[:, i * 128:(i + 1) * 128], r_v) for i, (kt, _) in enumerate(ents)]
                                if is_s:
                                    cvv = stg.views["cv"].rearrange("p (j n) -> p j n", j=4)
                                    tiles += [(cvv[:, j, :], Pt[:, 640 + j * 128:640 + (j + 1) * 128], stg.res["cv"]) for j in range(4)]
                                nt = len(tiles)
                                for i, (va, pa, rv) in enumerate(tiles):
                                    P.op("pe", (lambda h, i=i, va=va, pa=pa, ob=ob, nt=nt: h.matmul(ob[:, 0:128], lhsT=va, rhs=pa, start=(i == 0), stop=(i == nt - 1))),
                                         r=[rv, r_P[par]], w=[bank[b0 + 3]])
                                for i, (va, pa, rv) in enumerate(tiles):
                                    P.op("pe", (lambda h, i=i, pa=pa, ob=ob, nt=nt: h.matmul(ob[:, 128:256], lhsT=onesb[:, :], rhs=pa, start=(i == 0), stop=(i == nt - 1))),
                                         r=[r_P[par], R_const], w=[bank[b0 + 3]])
                                P.op("dve", (lambda h, par=par, ob=ob: h.reciprocal(out=rden[par][pb:pb + 64, :], in_=ob[pb:pb + 64, 128:256])),
                                     r=[bank[b0 + 3]], w=[r_rden[par]])
                                P.op("dve", (lambda h, par=par, ob=ob, t=t: h.tensor_tensor(out=ao[pb:pb + 64, c, t * 128:(t + 1) * 128], in0=ob[pb:pb + 64, 0:128],
                                                                                         in1=rden[par][pb:pb + 64, :], op=ALU.mult)),
                                     r=[bank[b0 + 3], r_rden[par]], w=[R_ao[t // 4]])
                loads = [ld(16 * 3 * 128, "wqkv", (lambda view, c=c: [(view.rearrange("p (k a n) -> p k a n", k=16, a=3), d_wqkv[l, c])]))]
                if is_s:
                    loads.append(ld(2 * 16 * 64, "bias", (lambda view, c=c: [(view.rearrange("p (a d c) -> p a d c", a=2, d=16), d_bias[l, c])])))
                    loads.append(ld(512, "ck", (lambda view, c=c: [(view, d_ckT[l, c])])))
                    loads.append(ld(512, "cv", (lambda view, c=c: [(view.rearrange("p (j n) -> p j n", j=4), d_cv[l, c])])))
                stages.append(Stage(loads, comp))
            return stages

        CVO_OFF = ARENA - 8192
        PLO_OFF = ARENA - 16384
        cvo = aview(CVO_OFF, 8192, BF16).rearrange("p (c t) -> p c t", c=4)
        plo = aview(PLO_OFF, 8192, BF16).rearrange("p (c t) -> p c t", c=4)
        R_cvo = [Res(), Res()]
        R_plo = [Res(), Res()]
        nseq = 1 if is_s else 4
        Ls = T // nseq

        def conv_stages(l):
            stages = []
            al = ArenaAlloc(0)
            HP = Ls + 30
            hpad = al.get(nseq * HP * 4, F32).rearrange("p (s n) -> p s n", s=nseq)
            c32t = al.get(16384, F32).rearrange("p (c t) -> p c t", c=4)
            sg = [al.get(2048, F32) for _ in range(2)]
            _sq = al.get(2048, F32)
            sq = [_sq, _sq]
            lnv = al.get(2048, F32)
            _rst = al.get(2048, F32)
            rst = [_rst, _rst]
            _yt = al.get(2048, F32)
            yt = [_yt, _yt]
            assert al.off <= CVO_OFF, al.off
            r_hpad = Res()
            r_c32 = [Res() for _ in range(4)]
            r_sg = [Res(), Res()]
            _r = Res()
            r_sq = [_r, _r]
            r_ln = Res()
            _r2 = Res()
            r_rst = [_r2, _r2]
            _r3 = Res()
            r_yt = [_r3, _r3]
            first = [True]

            for c in range(4):
                def comp(stg, c=c):
                    if first[0]:
                        first[0] = False
                        P.op("dve", lambda h: h.memset(hpad, 0.0), w=[r_hpad])
                    wc = stg.views["wc"].rearrange("p (k a n) -> p k a n", k=16, a=2)
                    for tt in range(NTT):
                        sl = slice(tt * TT, (tt + 1) * TT)
                        ba, bg = tt, 2 + tt
                        for which, bk in ((0, ba), (1, bg)):
                            for kc in range(16):
                                P.op("pe", (lambda h, kc=kc, which=which, bk=bk: h.matmul(ps[:, bk, :], lhsT=wc[:, kc, which, :], rhs=u[:, kc, sl],
                                                                                         start=(kc == 0), stop=(kc == 15))),
                                     r=[stg.res["wc"], R_u[tt]], w=[bank[bk]])
                        P.op("act", (lambda h, tt=tt, bg=bg: h.activation(out=sg[tt], in_=ps[:, bg, :], func=AF.Sigmoid)), r=[bank[bg]], w=[r_sg[tt]])
                        if is_s:
                            hv = hpad[:, 0, 15 + tt * TT:15 + (tt + 1) * TT]
                            av = ps[:, ba, :]
                            sv = sg[tt]
                        else:
                            hv = hpad[:, 2 * tt:2 * tt + 2, 15:15 + Ls]
                            av = ps[:, ba, :].rearrange("p (s n) -> p s n", s=2)
                            sv = sg[tt].rearrange("p (s n) -> p s n", s=2)
                        P.op("dve", (lambda h, hv=hv, av=av, sv=sv: h.tensor_tensor(out=hv, in0=av, in1=sv, op=ALU.mult)),
                             r=[bank[ba], r_sg[tt]], w=[r_hpad])
                    acc = c32t[:, c, :].rearrange("p (s n) -> p s n", s=nseq)
                    P.op("dve", (lambda h: h.tensor_scalar(out=acc, in0=hpad[:, :, 0:Ls], scalar1=cdw[:, l, c, 0:1], scalar2=cb[:, l, c:c + 1],
                                                           op0=ALU.mult, op1=ALU.add)), r=[r_hpad, R_const], w=[r_c32[c]])
                    for j in range(1, 31):
                        P.op("dve", (lambda h, j=j: h.scalar_tensor_tensor(out=acc, in0=hpad[:, :, j:j + Ls], scalar=cdw[:, l, c, j:j + 1], in1=acc,
                                                                          op0=ALU.mult, op1=ALU.add)), r=[r_hpad], w=[r_c32[c]])
                    for tt in range(NTT):
                        sl = slice(tt * TT, (tt + 1) * TT)
                        i = (2 * c + tt) % 2
                        P.op("act", (lambda h, i=i: h.activation(out=sq[i], in_=c32t[:, c, sl], func=AF.Square)), r=[r_c32[c]], w=[r_sq[i]])
                        P.op("pe", (lambda h, i=i, tt=tt: h.matmul(ps[:, 4 + tt, :], lhsT=ones32, rhs=sq[i], start=(c == 0), stop=(c == 3))),
                             r=[r_sq[i], R_const], w=[bank[4 + tt]])
                    if c == 3:
                        for tt in range(NTT):
                            sl = slice(tt * TT, (tt + 1) * TT)
                            P.op("act", (lambda h, tt=tt: h.activation(out=lnv, in_=ps[:, 4 + tt, :], func=AF.Ln, scale=1.0 / 512, bias=EPS)),
                                 r=[bank[4 + tt]], w=[r_ln])
                            P.op("act", (lambda h, tt=tt: h.activation(out=rst[tt], in_=lnv, func=AF.Exp, scale=-0.5)), r=[r_ln], w=[r_rst[tt]])
                            for cc in range(4):
                                i = cc % 2
                                P.op("dve", (lambda h, cc=cc, i=i, tt=tt: h.scalar_tensor_tensor(out=yt[i], in0=c32t[:, cc, sl], scalar=cg[:, l, cc:cc + 1], in1=rst[tt],
                                                                                                op0=ALU.mult, op1=ALU.mult)),
                                     r=[r_c32[cc], r_rst[tt]], w=[r_yt[i]])
                                P.op("act", (lambda h, cc=cc, i=i: h.activation(out=cvo[:, cc, sl], in_=yt[i], func=AF.Silu)), r=[r_yt[i]], w=[R_cvo[tt]])
                stages.append(Stage([ld(16 * 2 * 128, "wc", (lambda view, c=c: [(view.rearrange("p (k a n) -> p k a n", k=16, a=2), d_wconv[l, c])]))], comp))
            return stages

        def pool_stages(l):
            stages = []
            al = ArenaAlloc(0)
            ZP = Ls + 16
            zpad = al.get(nseq * ZP * 4, F32).rearrange("p (s n) -> p s n", s=nseq)
            sA = al.get(nseq * ZP * 4, F32).rearrange("p (s n) -> p s n", s=nseq)
            sB = al.get(nseq * ZP * 4, F32).rearrange("p (s n) -> p s n", s=nseq)
            dbf = al.get(2048, BF16)
            ict = al.get(4096, F32)
            r_ict = Res()
            d_inv = d_invS if is_s else d_invP
            assert al.off <= PLO_OFF
            r_z = Res()
            r_sA = Res()
            r_sB = Res()
            r_d = Res()
            r_tb = Res()
            first = [True]
            wins = (2, 4, 8, 16)
            for g in range(4):
                def comp(stg, g=g):
                    if first[0]:
                        first[0] = False
                        P.op("dve", lambda h: h.memset(zpad, 0.0), w=[r_z])
                    wp = stg.views["wp"].rearrange("p (k n) -> p k n", k=16)
                    pw = stg.views["pw"]
                    for tt in range(NTT):
                        sl = slice(tt * TT, (tt + 1) * TT)
                        for kc in range(16):
                            P.op("pe", (lambda h, kc=kc, tt=tt: h.matmul(ps[:, tt, :], lhsT=wp[:, kc, :], rhs=u[:, kc, sl], start=(kc == 0), stop=(kc == 15))),
                                 r=[stg.res["wp"], R_u[tt]], w=[bank[tt]])
                        if is_s:
                            zv = zpad[:, 0, 8 + tt * TT:8 + (tt + 1) * TT]
                            pv = ps[:, tt, :]
                        else:
                            zv = zpad[:, 2 * tt:2 * tt + 2, 8:8 + Ls]
                            pv = ps[:, tt, :].rearrange("p (s n) -> p s n", s=2)
                        P.op("act", (lambda h, zv=zv, pv=pv: h.activation(out=zv, in_=pv, func=AF.Copy)), r=[bank[tt]], w=[r_z])
                    w = wins[g]
                    P.op("dve", lambda h: h.tensor_tensor(out=sA[:, :, 1:ZP], in0=zpad[:, :, 0:ZP - 1], in1=zpad[:, :, 1:ZP], op=ALU.add),
                         r=[r_z], w=[r_sA])
                    cur, rcur, oth, roth = sA, r_sA, sB, r_sB
                    lo, hi = 1, ZP
                    sh = 1
                    for _ in range(g):
                        nlo, nhi = lo + sh, hi - sh
                        P.op("dve", (lambda h, cur=cur, oth=oth, nlo=nlo, nhi=nhi, sh=sh: h.tensor_tensor(out=oth[:, :, nlo:nhi], in0=cur[:, :, nlo - sh:nhi - sh],
                                                                                                      in1=cur[:, :, nlo + sh:nhi + sh], op=ALU.add)),
                             r=[rcur], w=[roth])
                        cur, rcur, oth, roth = oth, roth, cur, rcur
                        lo, hi = nlo, nhi
                        sh *= 2
                    Sv = cur[:, :, 8:8 + Ls]
                    zi = zpad[:, :, 8:8 + Ls]
                    dv = dbf.rearrange("p (s n) -> p s n", s=nseq)
                    P.dma("sp", (lambda h: h.dma_start(out=ict, in_=d_inv[:, g, :])), w=[r_ict],
                          extra=[P.eng[n].last for n in ("pe", "act", "dve")])
                    tmpv = oth[:, :, 8:8 + Ls]
                    P.op("dve", (lambda h: h.tensor_tensor(out=tmpv, in0=Sv, in1=ict.rearrange("p (s n) -> p s n", s=nseq), op=ALU.mult)),
                         r=[rcur, r_ict], w=[roth])
                    P.op("dve", (lambda h: h.tensor_tensor(out=dv, in0=tmpv, in1=zi, op=ALU.subtract)),
                         r=[roth, r_z], w=[r_d])
                    for tt in range(NTT):
                        sl = slice(tt * TT, (tt + 1) * TT)
                        P.op("pe", (lambda h, tt=tt: h.matmul(ps[:, 2 + tt, :], lhsT=pw, rhs=dbf[:, sl], start=True, stop=True)),
                             r=[stg.res["pw"], r_d], w=[bank[2 + tt]])
                        P.op("act", (lambda h, tt=tt: h.activation(out=plo[:, g, sl], in_=ps[:, 2 + tt, :], func=AF.Identity, scale=psc[:, l, g:g + 1], bias=0.0)),
                             r=[bank[2 + tt], R_const], w=[R_plo[tt]])
                loads = [ld(16 * 128, "wp", (lambda view, g=g: [(view.rearrange("p (k n) -> p k n", k=16), d_wpool[l, g])])),
                         ld(128, "pw", (lambda view, g=g: [(view, d_poolw[l, g])]))]
                stages.append(Stage(loads, comp))
            return stages

        def merge_stages(l):
            stages = []
            al = ArenaAlloc(0)
            mg = [al.get(1024, BF16) for _ in range(2)]
            sgt = [al.get(2048, F32) for _ in range(3)]
            mt = [al.get(2048, F32) for _ in range(3)]
            assert al.off <= PLO_OFF
            r_mg = [Res(), Res()]
            r_sgt = [Res() for _ in range(3)]
            r_mt = [Res() for _ in range(3)]
            for n in range(NCH):
                def comp(stg, n=n):
                    wg = stg.views["wg"].rearrange("p (k a n) -> p k a n", k=16, a=3)
                    wba = stg.views["wba"].rearrange("p (k n) -> p k n", k=8)
                    wbb = stg.views["wbb"].rearrange("p (k n) -> p k n", k=4)
                    wbc = stg.views["wbc"].rearrange("p (k n) -> p k n", k=4)
                    wo = stg.views["wo"]
                    brs = [(wba, 8, ao, R_ao, "wba"), (wbb, 4, cvo, R_cvo, "wbb"), (wbc, 4, plo, R_plo, "wbc")]
                    for tt in range(NTT):
                        sl = slice(tt * TT, (tt + 1) * TT)
                        for a in range(3):
                            bg = 2 * a
                            bb = 2 * a + 1
                            for kc in range(16):
                                P.op("pe", (lambda h, kc=kc, a=a, bg=bg: h.matmul(ps[:, bg, :], lhsT=wg[:, kc, a, :], rhs=u[:, kc, sl], start=(kc == 0), stop=(kc == 15))),
                                     r=[stg.res["wg"], R_u[tt]], w=[bank[bg]])
                            wb, nk, src, rsrc, key = brs[a]
                            for kc in range(nk):
                                P.op("pe", (lambda h, kc=kc, wb=wb, src=src, bb=bb, nk=nk: h.matmul(ps[:, bb, :], lhsT=wb[:, kc, :], rhs=src[:, kc, sl],
                                                                                               start=(kc == 0), stop=(kc == nk - 1))),
                                     r=[stg.res[key], rsrc[tt]], w=[bank[bb]])
                            P.op("act", (lambda h, a=a, bg=bg: h.activation(out=sgt[a], in_=ps[:, bg, :], func=AF.Sigmoid)), r=[bank[bg]], w=[r_sgt[a]])
                            P.op("dve", (lambda h, a=a, bb=bb: h.tensor_tensor(out=mt[a], in0=sgt[a], in1=ps[:, bb, :], op=ALU.mult)),
                                 r=[r_sgt[a], bank[bb]], w=[r_mt[a]])
                        P.op("dve", (lambda h: h.tensor_tensor(out=mt[0], in0=mt[0], in1=mt[1], op=ALU.add)), r=[r_mt[1]], w=[r_mt[0]])
                        P.op("dve", (lambda h, tt=tt: h.tensor_tensor(out=mg[tt], in0=mt[0], in1=mt[2], op=ALU.add)), r=[r_mt[0], r_mt[2]], w=[r_mg[tt]])
                    for tt in range(NTT):
                        sl = slice(tt * TT, (tt + 1) * TT)
                        for o in range(NCH):
                            bo = (6, 7, 0, 1, 2, 3, 4, 5)[(o + 8 * tt) % 8]
                            P.op("pe", (lambda h, o=o, bo=bo, tt=tt: h.matmul(ps[:, bo, :], lhsT=wo[:, o * 128:(o + 1) * 128], rhs=mg[tt], start=True, stop=True)),
                                 r=[stg.res["wo"], r_mg[tt]], w=[bank[bo]])
                            P.op("dve", (lambda h, o=o, bo=bo: h.scalar_tensor_tensor(out=x[:, o, sl], in0=ps[:, bo, :], scalar=Gv[:, o:o + 1], in1=x[:, o, sl],
                                                                                     op0=ALU.mult, op1=ALU.add)),
                                 r=[bank[bo], R_AG], w=[R_x[o][tt]])
                loads = [ld(16 * 3 * 128, "wg", (lambda view, n=n: [(view.rearrange("p (k a n) -> p k a n", k=16, a=3), d_wgate[l, n])])),
                         ld(8 * 128, "wba", (lambda view, n=n: [(view.rearrange("p (k n) -> p k n", k=8), d_wbra[l, n])])),
                         ld(4 * 128, "wbb", (lambda view, n=n: [(view.rearrange("p (k n) -> p k n", k=4), d_wbrb[l, n])])),
                         ld(4 * 128, "wbc", (lambda view, n=n: [(view.rearrange("p (k n) -> p k n", k=4), d_wbrc[l, n])])),
                         ld(D, "wo", (lambda view, n=n: [(view, d_wout[l, n])]))]
                stages.append(Stage(loads, comp))
            return stages

        def bar_stage():
            def comp(stg):
                P.barrier(list(arena_reads))
                del arena_reads[:]
            return Stage([], comp)

        stages = []
        for l in range(NL):
            stages.append(norm_stage(l, 0))
            stages += ffn_stages(l, 0)
            stages.append(norm_stage(l, 1))
            stages.append(bar_stage())
            pend = list(mods_extra[l + 1]) if (mods_extra is not None and l + 1 < NL) else []
            for st in attn_stages(l):
                stages.append(st)
                for _ in range(3):
                    if pend:
                        stages.append(pend.pop(0))
            stages.append(bar_stage())
            stages += conv_stages(l)
            stages.append(bar_stage())
            stages += pool_stages(l)
            stages.append(bar_stage())
            for st in merge_stages(l):
                stages.append(st)
                for _ in range(3):
                    if pend:
                        stages.append(pend.pop(0))
            stages += pend
            stages.append(bar_stage())
            stages.append(norm_stage(l, 2))
            stages += ffn_stages(l, 1)
        run_stages(stages[:cfg.get('nst', 100000)])
        for n in range(NCH):
            for tt in range(NTT):
                ev = P.dma("sp", (lambda h, n=n, tt=tt: h.dma_start(out=d_y[n, :, tt * TT:(tt + 1) * TT], in_=x[:, n, tt * TT:(tt + 1) * TT])),
                           r=[R_x[n][tt]])
                P.out_events.append(ev)

    ms = mods_stages()
    if groups:
        run_stages(ms[0])
        for gi_, g in enumerate(groups):
            group_pass(g, ms if gi_ == 0 else None)
    else:
        for m_ in ms:
            run_stages(m_)
    P._waits("sp", P.out_events)
    with nc.Block() as block:
        @block.tensor
        def _(h):
            P.replay("pe", h)

        @block.scalar
        def _(h):
            P.replay("act", h)

        @block.vector
        def _(h):
            P.replay("dve", h)

        @block.gpsimd
        def _(h):
            P.replay("pool", h)

        @block.sync
        def _(h):
            P.replay("sp", h)
    es.close()
    return nc


def _fm(a):
    t, d = a.shape
    return np.ascontiguousarray(a.T.reshape(d // 128, 128, t))


def _vec_fm(v, nchunks):
    lead = v.shape[:-1]
    r = v.reshape(lead + (nchunks, 128))
    return np.ascontiguousarray(np.moveaxis(r, -1, 0))


def prep_shared(inp):
    f = np.float32
    sh = {}
    L = inp["w_mod"].shape[0]
    w_mod = inp["w_mod"]
    sh["wmod"] = np.ascontiguousarray(w_mod.reshape(L, 16, 128, 72, 256).transpose(0, 3, 2, 1, 4))
    bm = _vec_fm(inp["b_mod"], 144)
    sh["bmod"] = np.ascontiguousarray(np.repeat(bm[..., None], 2, axis=-1))
    sh["ng"] = _vec_fm(inp["norm_g"], 16)
    w1 = inp["ffn_w1"]
    w3 = inp["ffn_w3"]
    w2 = inp["ffn_w2"]
    w13 = np.zeros((L, 2, NST, 128, 2, 16, 256), f)
    PADF = NST * 256
    for a, w in ((0, w1), (1, w3)):
        wp = np.zeros((L, 2, D, PADF), f)
        wp[..., :DFF] = w
        w13[:, :, :, :, a] = wp.reshape(L, 2, 16, 128, NST, 256).transpose(0, 1, 4, 3, 2, 5)
        del wp
    sh["w13"] = w13
    w2p = np.zeros((L, 2, PADF, D), f)
    w2p[:, :, :DFF] = w2
    sh["w2"] = np.ascontiguousarray(w2p.reshape(L, 2, NST, 2, 128, D).transpose(0, 1, 2, 4, 3, 5))
    del w2p
    w_in = inp["w_in"].reshape(L, 16, 128, 10752)
    qkv = w_in[..., 0:3072].reshape(L, 16, 128, 3, 8, 128)
    sh["wqkv"] = np.ascontiguousarray(qkv.transpose(0, 4, 2, 1, 3, 5))
    cv = w_in[..., 3072:4096].reshape(L, 16, 128, 2, 4, 128)
    sh["wconv"] = np.ascontiguousarray(cv.transpose(0, 4, 2, 1, 3, 5))
    pl = w_in[..., 4096:4608].reshape(L, 16, 128, 4, 128)
    sh["wpool"] = np.ascontiguousarray(pl.transpose(0, 3, 2, 1, 4))
    gt = w_in[..., 4608:10752].reshape(L, 16, 128, 3, 16, 128)
    sh["wgate"] = np.ascontiguousarray(gt.transpose(0, 4, 2, 1, 3, 5))
    sh["wbra"] = np.ascontiguousarray(inp["w_br_a"].reshape(L, 8, 128, 16, 128).transpose(0, 3, 2, 1, 4))
    sh["wbrb"] = np.ascontiguousarray(inp["w_br_b"].reshape(L, 4, 128, 16, 128).transpose(0, 3, 2, 1, 4))
    sh["wbrc"] = np.ascontiguousarray(inp["w_br_c"].reshape(L, 4, 128, 16, 128).transpose(0, 3, 2, 1, 4))
    sh["wout"] = np.ascontiguousarray(inp["w_out"].reshape(L, 16, 128, D))
    sh["poolw"] = np.ascontiguousarray(inp["pool_w"])
    qk = inp["qk_g"]
    sh["qkg"] = np.ascontiguousarray(np.concatenate([qk, qk], axis=-1).transpose(2, 0, 1))
    sh["cdw"] = np.ascontiguousarray(inp["conv_dw"].reshape(L, 31, 4, 128).transpose(3, 0, 2, 1))
    sh["cb"] = _vec_fm(inp["conv_b"], 4)
    sh["cg"] = _vec_fm(inp["conv_g"], 4)
    sh["psc"] = _vec_fm(inp["pool_scale"], 4)
    rpb = inp["rpb"]
    kc = np.arange(64)[:, None]
    cc = np.arange(64)[None, :]
    cstart = np.clip(cc - 8, 0, 48)
    colok = (kc >= cstart) & (kc < cstart + 16)
    dcidx = np.clip(kc - cc, -15, 15) + 15
    tab = np.full((L, 16, 2, 64, 16, 64), NEG, f)
    for half in range(2):
        for d in range(16):
            dr = (7 - d) if half == 0 else (8 - d)
            if -7 <= dr <= 7:
                m = rpb[:, :, dr + 7, :][:, :, dcidx]
                tab[:, :, half, :, d, :] = np.where(colok[None, None], m, f(NEG))
    tab = tab.reshape(L, 8, 2, 128, 16, 64).transpose(0, 1, 3, 2, 4, 5)
    sh["biasT"] = np.ascontiguousarray(tab)
    pats = mask_patterns()
    ml = np.zeros((2, len(pats), 128), f)
    for pi, pat in enumerate(pats):
        for prp in range(2):
            for krp in range(2):
                if pat[krp][prp]:
                    ml[prp, pi, krp * 64:(krp + 1) * 64] = -32768.0
    sh["maskl"] = ml
    indm = np.zeros((2, 128), f)
    indm[0, 0:64] = 1.0
    indm[1, 64:128] = 1.0
    sh["ind"] = indm
    c32 = np.zeros((128, 256), f)
    c32[:, 0:128] = 1.0
    c32[0:64, 128:192] = 1.0
    c32[64:128, 192:256] = 1.0
    sh["c32"] = c32
    return sh


def inv_table(Ls):
    tab = np.zeros((128, 4, T), np.float32)
    t = np.arange(Ls)
    for g, w in enumerate((2, 4, 8, 16)):
        lo = np.clip(t - w // 2, 0, Ls)
        hi = np.clip(t - w // 2 + w, 0, Ls)
        inv = (1.0 / (hi - lo)).astype(np.float32)
        tab[:, g, :] = np.tile(inv, T // Ls)[None, :]
    return tab


def cnt_table(Ls):
    tab = np.ones((128, 4, 4, 16), np.float32)
    for g, w in enumerate((2, 4, 8, 16)):
        for t in range(w // 2):
            tab[:, g, :, t] = 1.0 / (t + w // 2)
        nr = w // 2 - 1
        for i in range(nr):
            t = Ls - nr + i
            tab[:, g, :, 8 + i] = 1.0 / (Ls - t + w // 2)
    return tab


_NC_CACHE = {}


def kernel(**inp):
    inp = {k: np.asarray(v) for k, v in inp.items()}
    NL = DEBUG["NL"]
    ncores = DEBUG.get("ncores", 8)
    if NL < L:
        for k in ("w_mod", "b_mod", "norm_g", "ffn_w1", "ffn_w3", "ffn_w2", "w_in", "qk_g", "rpb", "w_br_a", "conv_dw",
                  "conv_b", "conv_g", "w_br_b", "pool_w", "pool_scale", "w_br_c", "w_out"):
            inp[k] = inp[k][:NL]
        inp["cache_k"] = inp["cache_k"][:, :NL]
        inp["cache_v"] = inp["cache_v"][:, :NL]
    sh = prep_shared(inp)
    key = (DEBUG["NL"], tuple(DEBUG["groups"]), DEBUG.get("stop"), DEBUG.get("nmods"), DEBUG.get("nst"), DEBUG.get("attpart"), DEBUG.get("vsub"))
    import time as _t
    _t0 = _t.time()
    if key not in _NC_CACHE:
        _NC_CACHE[key] = build_nc(DEBUG)
    nc = _NC_CACHE[key]
    xp = inp["x_prompt"]
    xs = inp["x_sample"]
    ck = inp["cache_k"]
    cv = inp["cache_v"]
    in_maps = []
    for c in range(ncores):
        b = c // 2
        m = dict(sh)
        m["xs"] = _fm(xs[b])
        m["xp"] = _fm(xp[4 * c:4 * c + 4].reshape(T, D))
        cvec = np.stack([inp["c"][b], inp["c_ctx"]], axis=-1)
        m["cvec"] = np.ascontiguousarray(cvec.reshape(16, 128, 2).transpose(1, 0, 2))
        ckb = inp["cache_k"][b]
        m["ckT"] = np.ascontiguousarray(ckb.reshape(NL, 512, 8, 128).transpose(0, 2, 3, 1))
        cvb = inp["cache_v"][b]
        m["cvp"] = np.ascontiguousarray(cvb.reshape(NL, 4, 128, 8, 128).transpose(0, 3, 2, 1, 4))
        m["cnt"] = np.concatenate([cnt_table(1024)[:, :, 0:0], cnt_table(256)], axis=2) if False else cnt_table(256)
        m["invS"] = inv_table(1024)
        m["invP"] = inv_table(256)
        in_maps.append(m)
    print('host prep s', _t.time() - _t0, flush=True)
    res = run_bass_kernel_spmd(nc, in_maps, core_ids=list(range(ncores)))
    print('run s', _t.time() - _t0, flush=True)
    outs = res.results
    B, S = 32, 256
    yp = np.zeros((B, S, D), np.float32)
    ys = np.zeros((4, 1024, D), np.float32)
    nk = np.zeros((B, L, S, 16, 64), np.float32)
    nv = np.zeros((B, L, S, 16, 64), np.float32)
    for c in range(ncores):
        o = outs[c]
        yp[4 * c:4 * c + 4] = o["yp"].reshape(D, T).T.reshape(4, S, D)
        if c % 2 == 0:
            ys[c // 2] = o["ys"].reshape(D, T).T
        k = o["nk"].transpose(3, 0, 1, 2).reshape(4, S, NL, 16, 64)
        nk[4 * c:4 * c + 4, :NL] = k.transpose(0, 2, 1, 3, 4)
        v = o["nv"].transpose(3, 2, 0, 1, 4).reshape(T, NL, 1024).reshape(4, S, NL, 16, 64)
        nv[4 * c:4 * c + 4, :NL] = v.transpose(0, 2, 1, 3, 4)
    return (yp, ys, nk, nv)
```

```python
import numpy as np
from contextlib import ExitStack
import concourse.bass as bass
import concourse.mybir as mybir
from concourse.bass_utils import run_bass_kernel_spmd

F32 = mybir.dt.float32
BF16 = mybir.dt.bfloat16
AF = mybir.ActivationFunctionType
ALU = mybir.AluOpType

D = 2048
NCH = 16
T = 1024
NTT = 2
TT = 512
L = 4
DFF = 5504
NFC = 43
NST = 22
H = 16
EPS = 1e-6
NEG = -30000.0
RING = 20480
ARENA = 43008
N_DMA_SEMS = 8

DEBUG = {"NL": L, "groups": ("S", "P")}


class Res:
    __slots__ = ("name", "w", "rs")

    def __init__(self, name=""):
        self.name = name
        self.w = None
        self.rs = {}


class _Eng:
    def __init__(self, name):
        self.name = name
        self.items = []
        self.seq = 0
        self.clock = {}
        self.sems = []
        self.last = None


class _Rec:
    def __init__(self):
        self.call = None

    def __getattr__(self, name):
        def f(*a, **k):
            assert self.call is None
            self.call = (name, a, k)
            return self
        return f


def _record(fn):
    rec = _Rec()
    fn(rec)
    assert rec.call is not None
    return rec.call


class Prog:
    EPOCH = 30000

    def __init__(self, nc, es):
        self.nc = nc
        self.es = es
        self.eng = {n: _Eng(n) for n in ("pe", "act", "dve", "pool", "sp")}
        self.dsem = {}
        self.dval = {}
        self.dlast = {}
        self.drr = {"pool": 0, "sp": 0}
        for q in ("pool", "sp"):
            for i in range(N_DMA_SEMS):
                k = (q, i)
                self.dsem[k] = es.enter_context(nc.semaphore("d_%s_%d" % (q, i)))
                self.dval[k] = 0
                self.dlast[k] = None
        self.out_events = []

    def _sem_for(self, e, seq):
        idx = (seq - 1) // self.EPOCH
        while len(e.sems) <= idx:
            e.sems.append(self.es.enter_context(self.nc.semaphore("e_%s_%d" % (e.name, len(e.sems)))))
        return e.sems[idx], seq - idx * self.EPOCH

    @staticmethod
    def _deps(reads, writes):
        evs = []
        for r in reads:
            if r.w is not None:
                evs.append(r.w)
        for w in writes:
            if w.w is not None:
                evs.append(w.w)
            evs.extend(w.rs.values())
        return evs

    def _waits(self, X, evs):
        e = self.eng[X]
        ck = e.clock
        for ev in evs:
            key, val = ev[0], ev[1]
            if key == X:
                continue
            if ck.get(key, 0) >= val:
                continue
            e.items.append(("w", ev[3], ev[4]))
            for k, v in ev[2].items():
                if ck.get(k, 0) < v:
                    ck[k] = v
            if ck.get(key, 0) < val:
                ck[key] = val

    def op(self, X, fn, r=(), w=()):
        e = self.eng[X]
        self._waits(X, self._deps(r, w))
        e.seq += 1
        sem, sv = self._sem_for(e, e.seq)
        e.items.append(("i", _record(fn), sem))
        clock = dict(e.clock)
        clock[X] = e.seq
        ev = (X, e.seq, clock, sem, sv)
        e.last = ev
        for rr in r:
            rr.rs[X] = ev
        for ww in w:
            ww.w = ev
            ww.rs = {}
        return ev

    def dma(self, Q, fn, r=(), w=(), extra=()):
        e = self.eng[Q]
        i = self.drr[Q]
        self.drr[Q] = (i + 1) % N_DMA_SEMS
        k = (Q, i)
        evs = self._deps(r, w) + [x for x in extra if x is not None]
        if self.dlast[k] is not None:
            evs.append(self.dlast[k])
        self._waits(Q, evs)
        self.dval[k] += 16
        val = self.dval[k]
        sem = self.dsem[k]
        e.items.append(("d", _record(fn), sem))
        key = ("d",) + k
        clock = dict(e.clock)
        ev = (key, val, clock, sem, val)
        self.dlast[k] = ev
        for rr in r:
            rr.rs[key] = ev
        for ww in w:
            ww.w = ev
            ww.rs = {}
        return ev

    def barrier(self, extra_events=()):
        lasts = [self.eng[n].last for n in ("pe", "act", "dve") if self.eng[n].last is not None]
        for X in ("pe", "act", "dve"):
            self._waits(X, list(lasts) + list(extra_events))

    def replay(self, X, h):
        for it in self.eng[X].items:
            if it[0] == "w":
                h.wait_ge(it[1], it[2])
            elif it[0] == "i":
                getattr(h, it[1][0])(*it[1][1], **it[1][2]).then_inc(it[2], 1)
            else:
                getattr(h, it[1][0])(*it[1][1], **it[1][2]).then_inc(it[2], 16)


def _row_start(r):
    return min(max(r - 4, 0), 8)


def sample_attn_plan():
    plan = []
    for t in range(8):
        rows = (2 * t, 2 * t + 1)
        need = set()
        for r in rows:
            s = _row_start(r)
            need.update(range(s, s + 8))
        kts = sorted({kr // 2 for kr in need}, reverse=True)
        ents = []
        for kt in kts:
            W = [[False, False], [False, False]]
            anym = False
            for krp in range(2):
                kr = 2 * kt + krp
                for pr in range(2):
                    s = _row_start(rows[pr])
                    ok = s <= kr <= s + 7
                    W[krp][pr] = not ok
                    anym = anym or (not ok)
            ents.append((kt, (tuple(W[0]), tuple(W[1])) if anym else None))
        plan.append(ents)
    return plan


def mask_patterns():
    pats = []
    for ents in sample_attn_plan():
        for _, p in ents:
            if p is not None and p not in pats:
                pats.append(p)
    return pats


class Stage:
    def __init__(self, loads, compute):
        self.loads = loads
        self.compute = compute
        self.res = {}
        self.views = {}
        self.blocks = {}

    def done(self, key):
        self.blocks[key][3] = True

    def done_all(self):
        for b in self.blocks.values():
            b[3] = True


def build_nc(cfg=None):
    cfg = cfg or DEBUG
    NL = cfg["NL"]
    groups = cfg["groups"]
    LW = NL
    nc = bass.Bass("TRN2", target_bir_lowering=False)
    es = ExitStack()

    def din(name, shape):
        return nc.dram_tensor(name, list(shape), F32, kind="ExternalInput").ap()

    def dout(name, shape):
        return nc.dram_tensor(name, list(shape), F32, kind="ExternalOutput").ap()

    npat = len(mask_patterns())
    d_xs = din("xs", [NCH, 128, T])
    d_xp = din("xp", [NCH, 128, T])
    d_cvec = din("cvec", [128, NCH, 2])
    d_wmod = din("wmod", [LW, 72, 128, 16, 256])
    d_bmod = din("bmod", [128, LW, 144, 2])
    d_ng = din("ng", [128, LW, 3, NCH])
    d_w13 = din("w13", [LW, 2, NST, 128, 2, 16, 256])
    d_w2 = din("w2", [LW, 2, NST, 128, 2, D])
    d_wqkv = din("wqkv", [LW, 8, 128, 16, 3, 128])
    d_wconv = din("wconv", [LW, 4, 128, 16, 2, 128])
    d_wpool = din("wpool", [LW, 4, 128, 16, 128])
    d_wgate = din("wgate", [LW, 16, 128, 16, 3, 128])
    d_wbra = din("wbra", [LW, 16, 128, 8, 128])
    d_wbrb = din("wbrb", [LW, 16, 128, 4, 128])
    d_wbrc = din("wbrc", [LW, 16, 128, 4, 128])
    d_wout = din("wout", [LW, 16, 128, D])
    d_poolw = din("poolw", [LW, 4, 128, 128])
    d_qkg = din("qkg", [128, LW, 2])
    d_cdw = din("cdw", [128, LW, 4, 31])
    d_cb = din("cb", [128, LW, 4])
    d_cg = din("cg", [128, LW, 4])
    d_psc = din("psc", [128, LW, 4])
    d_bias = din("biasT", [LW, 8, 128, 2, 16, 64])
    d_ckT = din("ckT", [LW, 8, 128, 512])
    d_cv = din("cvp", [LW, 8, 128, 4, 128])
    d_maskl = din("maskl", [2, npat, 128])
    d_ind = din("ind", [2, 128])
    d_cnt = din("cnt", [128, 4, 4, 16])
    d_invS = din("invS", [128, 4, T])
    d_invP = din("invP", [128, 4, T])
    d_c32 = din("c32", [128, 256])

    o_ys = dout("ys", [NCH, 128, T])
    o_yp = dout("yp", [NCH, 128, T])
    o_nk = dout("nk", [LW, 8, 128, T])
    o_nv = dout("nv", [LW, 8, 128, 8, 128])

    def sb(name, shape, dt):
        return es.enter_context(nc.sbuf_tensor("s_" + name, list(shape), dt))

    x = sb("x", [128, NCH, T], F32)
    u = sb("u", [128, NCH, T], BF16)
    ring = sb("ring", [128, RING], BF16)
    ao = sb("ao", [128, 8, T], BF16)
    arena = sb("arena", [128, ARENA // 4], F32)
    mods = sb("mods", [128, LW, 144, 2], F32)
    ng = sb("ng", [128, LW, 3, NCH], F32)
    qkg = sb("qkg", [128, LW, 2], F32)
    cdw = sb("cdw", [128, LW, 4, 31], F32)
    cb = sb("cb", [128, LW, 4], F32)
    cg = sb("cg", [128, LW, 4], F32)
    psc = sb("psc", [128, LW, 4], F32)
    c32 = sb("c32", [128, 256], F32)
    onesb = sb("onesb", [128, 128], BF16)
    maskl = sb("maskl", [2, npat, 128], BF16)
    ind = sb("ind", [2, 128], BF16)
    cnt = sb("cnt", [128, 4, 4, 16], F32)
    cvec = sb("cvec", [128, NCH, 2], F32)
    sT = sb("sT", [128, NCH, 2], BF16)
    Av = sb("Av", [128, NCH], F32)
    Gv = sb("Gv", [128, NCH], F32)
    qg8 = sb("qg8", [128, LW], F32)
    ps = es.enter_context(nc.psum_tensor("ps", [128, 8, 512], F32))

    P = Prog(nc, es)
    bank = [Res("bank%d" % i) for i in range(8)]
    R_x = [[Res("x%d_%d" % (n, tt)) for tt in range(NTT)] for n in range(NCH)]
    R_u = [Res("u%d" % tt) for tt in range(NTT)]
    R_ao = [Res("ao%d" % tt) for tt in range(NTT)]
    R_const = Res("const")
    R_mods = Res("mods")
    R_AG = Res("AG")

    def aview(off, nbytes, dt, shape=None):
        assert off % 4 == 0 and nbytes % 4 == 0 and off + nbytes <= ARENA, (off, nbytes)
        v = arena[:, off // 4:(off + nbytes) // 4]
        if dt == BF16:
            v = v.bitcast(BF16)
        return v

    class ArenaAlloc:
        def __init__(self, base=0):
            self.off = base

        def get(self, nbytes, dt):
            v = aview(self.off, nbytes, dt)
            self.off += nbytes
            return v

    ring_state = {"ptr": 0, "live": []}

    def ring_try_alloc(st, size):
        ptr = st["ptr"]
        if ptr + size > RING:
            ptr = 0
        lo, hi = ptr, ptr + size
        over = [b for b in st["live"] if not (b[0] + b[1] <= lo or b[0] >= hi)]
        for b in over:
            if not b[3]:
                return None
        for b in over:
            st["live"].remove(b)
        st["ptr"] = hi
        return lo, over

    def stage_try_load(stg):
        if not stg.loads:
            return True
        trial = {"ptr": ring_state["ptr"], "live": list(ring_state["live"])}
        for (n, fn, key) in stg.loads:
            got = ring_try_alloc(trial, n)
            if got is None:
                return False
            trial["live"].append([got[0], n, None, False])
        for (n, fn, key) in stg.loads:
            off, over = ring_try_alloc(ring_state, n)
            old = [b[2] for b in over]
            r = Res(key)
            blk = [off, n, r, False]
            ring_state["live"].append(blk)
            stg.blocks[key] = blk
            view = ring[:, off:off + n]
            stg.res[key] = r
            stg.views[key] = view
            pairs = fn(view)
            assert len(pairs) == 1
            oa, ia = pairs[0]
            P.dma("pool", (lambda h, oa=oa, ia=ia: h.dma_start(out=oa, in_=ia)), w=[r] + old)
        return True

    def run_stages(stages, ahead=8):
        n = len(stages)
        li = 0
        ci = 0
        while ci < n:
            while li < n and li - ci < ahead:
                if not stage_try_load(stages[li]):
                    break
                li += 1
            assert li > ci, "ring too small for stage"
            stages[ci].compute(stages[ci])
            stages[ci].done_all()
            ci += 1

    def ld(n, key, out_fn):
        return (n, out_fn, key)

    def sp_load(dst, src, res):
        P.dma("sp", (lambda h: h.dma_start(out=dst, in_=src)), w=[res])

    small = [(mods[:], d_bmod), (ng[:], d_ng), (qkg[:], d_qkg), (cdw[:], d_cdw), (cb[:], d_cb),
             (cg[:], d_cg), (psc[:], d_psc), (c32[:], d_c32), (cnt[:], d_cnt), (cvec[:], d_cvec)]
    R_small = [Res("small%d" % i) for i in range(len(small))]
    for (dst, src), r in zip(small, R_small):
        sp_load(dst, src, r)
    R_m2 = [Res("maskl"), Res("ind")]
    P.dma("pool", lambda h: h.dma_start(out=maskl[:], in_=d_maskl), w=[R_m2[0]])
    P.dma("pool", lambda h: h.dma_start(out=ind[:], in_=d_ind), w=[R_m2[1]])
    ones32 = c32[:, 0:128]
    bones32 = c32[:, 128:256]
    P.op("dve", lambda h: h.memset(onesb[:], 1.0), w=[R_const])
    P.op("act", lambda h: h.activation(out=sT[:], in_=cvec[:], func=AF.Silu), r=[R_small[9]], w=[R_const])
    P.op("dve", lambda h: h.tensor_scalar(out=qg8[:], in0=qkg[:, :, 0], scalar1=0.125, scalar2=None, op0=ALU.mult),
         r=[R_small[2]], w=[R_const])
    P.barrier([r.w for r in R_small] + [r.w for r in R_m2])

    def mods_stages():
        per_layer = []
        for l in range(NL):
            stages = []
            per_layer.append(stages)
            for s in range(72):
                def comp(stg, l=l, s=s):
                    wv = stg.views["w"].rearrange("p (k n) -> p k n", k=16)
                    b = bank[s % 2]
                    pv = ps[:, s % 2, 0:4]
                    for j in range(2):
                        for kc in range(16):
                            P.op("pe", (lambda h, j=j, kc=kc: h.matmul(pv[:, 2 * j:2 * j + 2], lhsT=wv[:, kc, j * 128:(j + 1) * 128],
                                                                         rhs=sT[:, kc, :], start=(kc == 0), stop=(kc == 15))),
                                 r=[stg.res["w"], R_const], w=[b])
                    mv = mods[:, l, 2 * s:2 * s + 2, :]
                    P.op("dve", (lambda h: h.tensor_tensor(out=mv, in0=pv.rearrange("p (a b) -> p a b", a=2), in1=mv, op=ALU.add)),
                         r=[b], w=[R_mods])
                stages.append(Stage([ld(4096, "w", (lambda view, l=l, s=s: [(view.rearrange("p (k n) -> p k n", k=16), d_wmod[l, s])]))], comp))
        return per_layer

    def group_pass(gname, mods_extra=None):
        gi = 0 if gname == "S" else 1
        is_s = gname == "S"
        d_x = d_xs if is_s else d_xp
        d_y = o_ys if is_s else o_yp
        plan = sample_attn_plan()
        pats = mask_patterns()
        arena_reads = []

        for n in range(NCH):
            for tt in range(NTT):
                P.dma("sp", (lambda h, n=n, tt=tt: h.dma_start(out=x[:, n, tt * TT:(tt + 1) * TT], in_=d_x[n, :, tt * TT:(tt + 1) * TT])),
                      w=[R_x[n][tt]])

        def mod_view(l, sub, which, c=None):
            base = (3 * sub + which) * 16
            if c is None:
                return mods[:, l, base:base + 16, gi]
            return mods[:, l, base + c, gi:gi + 1]

        _al = ArenaAlloc(0)
        n_sq = [_al.get(2048, F32) for _ in range(2)]
        n_tn = [_al.get(2048, F32) for _ in range(2)]
        n_lnv = _al.get(2048, F32)
        n_rstd = [_al.get(2048, F32) for _ in range(2)]
        nr_sq = [Res(), Res()]
        nr_tn = [Res(), Res()]
        nr_ln = Res()
        nr_rstd = [Res(), Res()]
        def norm_stage(l, sub):
            def comp(stg):
                sq, tn, lnv, rstd = n_sq, n_tn, n_lnv, n_rstd
                r_sq, r_tn, r_ln, r_rstd = nr_sq, nr_tn, nr_ln, nr_rstd
                P.op("dve", (lambda h: h.scalar_tensor_tensor(out=Av[:], in0=mod_view(l, sub, 1), scalar=1.0, in1=ng[:, l, sub, :],
                                                                op0=ALU.add, op1=ALU.mult)), r=[R_mods], w=[R_AG])
                gsc = 1.0 if sub == 1 else 0.5
                P.op("dve", (lambda h: h.tensor_scalar(out=Gv[:], in0=mod_view(l, sub, 2), scalar1=gsc, scalar2=None, op0=ALU.mult)),
                     r=[R_mods], w=[R_AG])
                for tt in range(NTT):
                    sl = slice(tt * TT, (tt + 1) * TT)
                    bk = bank[7]
                    for c in range(NCH):
                        i = c % 2
                        P.op("act", (lambda h, c=c, i=i: h.activation(out=sq[i], in_=x[:, c, sl], func=AF.Square)),
                             r=[R_x[c][tt]], w=[r_sq[i]])
                        P.op("pe", (lambda h, c=c, i=i: h.matmul(ps[:, 7, :], lhsT=ones32, rhs=sq[i], start=(c == 0), stop=(c == NCH - 1))),
                             r=[r_sq[i], R_const], w=[bk])
                    P.op("act", (lambda h: h.activation(out=lnv, in_=ps[:, 7, :], func=AF.Ln, scale=1.0 / D, bias=EPS)), r=[bk], w=[r_ln])
                    P.op("act", (lambda h, tt=tt: h.activation(out=rstd[tt], in_=lnv, func=AF.Exp, scale=-0.5)), r=[r_ln], w=[r_rstd[tt]])
                    for c in range(NCH):
                        i = c % 2
                        P.op("dve", (lambda h, c=c, i=i, tt=tt: h.scalar_tensor_tensor(out=tn[i], in0=x[:, c, sl], scalar=Av[:, c:c + 1], in1=rstd[tt],
                                                                                      op0=ALU.mult, op1=ALU.mult)),
                             r=[R_x[c][tt], R_AG, r_rstd[tt]], w=[r_tn[i]])
                        P.op("act", (lambda h, c=c, i=i: h.activation(out=u[:, c, sl], in_=tn[i], func=AF.Identity,
                                                                      bias=mod_view(l, sub, 0, c), scale=1.0)),
                             r=[r_tn[i], R_mods], w=[R_u[tt]])
            return Stage([], comp)

        ffn_base = 14336
        g_tiles = [aview(ffn_base + i * 4096, 4096, BF16).rearrange("p (j t) -> p j t", j=2) for i in range(2)]
        sa_t = [aview(ffn_base + 8192 + i * 2048, 2048, F32) for i in range(2)]
        r_g = [[Res(), Res()] for _ in range(2)]
        r_sa = [Res(), Res()]

        def ffn_stages(l, f):
            stages = []
            for s in range(NST):
                J = 2 if s < NST - 1 else 1

                def comp_up(stg, s=s, J=J):
                    w13 = stg.views["w13"].rearrange("p (a k n) -> p a k n", a=2, k=16)
                    gt = g_tiles[s % 2]
                    rg = r_g[s % 2]
                    for tt in range(NTT):
                        sl = slice(tt * TT, (tt + 1) * TT)
                        for jj in range(J):
                            ba = jj % 2
                            bb = 2 + jj % 2
                            for kc in range(16):
                                P.op("pe", (lambda h, kc=kc, jj=jj, ba=ba: h.matmul(ps[:, ba, :], lhsT=w13[:, 0, kc, jj * 128:(jj + 1) * 128], rhs=u[:, kc, sl],
                                                                                   start=(kc == 0), stop=(kc == 15))),
                                     r=[stg.res["w13"], R_u[tt]], w=[bank[ba]])
                            for kc in range(16):
                                P.op("pe", (lambda h, kc=kc, jj=jj, bb=bb: h.matmul(ps[:, bb, :], lhsT=w13[:, 1, kc, jj * 128:(jj + 1) * 128], rhs=u[:, kc, sl],
                                                                                   start=(kc == 0), stop=(kc == 15))),
                                     r=[stg.res["w13"], R_u[tt]], w=[bank[bb]])
                            si = jj % 2
                            P.op("act", (lambda h, ba=ba, si=si: h.activation(out=sa_t[si], in_=ps[:, ba, :], func=AF.Silu)),
                                 r=[bank[ba]], w=[r_sa[si]])
                            P.op("dve", (lambda h, bb=bb, si=si, jj=jj: h.tensor_tensor(out=gt[:, jj, sl], in0=sa_t[si], in1=ps[:, bb, :], op=ALU.mult)),
                                 r=[r_sa[si], bank[bb]], w=[rg[tt]])

                stages.append(Stage([ld(2 * 16 * 256, "w13", (lambda view, s=s: [(view.rearrange("p (a k n) -> p a k n", a=2, k=16), d_w13[l, f, s])]))], comp_up))
                if s % 2 == 1:
                    def comp_dn2(stg, s0=s - 1):
                        parts = []
                        for si, key in ((s0, "w2a"), (s0 + 1, "w2b")):
                            Jx = 2 if si < NST - 1 else 1
                            w2 = stg.views[key].rearrange("p (j n) -> p j n", j=2)
                            for jj in range(Jx):
                                parts.append((w2, jj, g_tiles[si % 2], r_g[si % 2], key))
                        npart = len(parts)
                        for tt in range(NTT):
                            sl = slice(tt * TT, (tt + 1) * TT)
                            for n in range(NCH):
                                bo = 4 + (n % 4)
                                for pi, (w2, jj, gt, rg, key) in enumerate(parts):
                                    P.op("pe", (lambda h, n=n, bo=bo, w2=w2, jj=jj, gt=gt, pi=pi: h.matmul(ps[:, bo, :], lhsT=w2[:, jj, n * 128:(n + 1) * 128], rhs=gt[:, jj, sl],
                                                                                                         start=(pi == 0), stop=(pi == npart - 1))),
                                         r=[stg.res[key], rg[tt]], w=[bank[bo]])
                                P.op("dve", (lambda h, n=n, bo=bo: h.scalar_tensor_tensor(out=x[:, n, sl], in0=ps[:, bo, :], scalar=Gv[:, n:n + 1], in1=x[:, n, sl],
                                                                                         op0=ALU.mult, op1=ALU.add)),
                                     r=[bank[bo], R_AG], w=[R_x[n][tt]])
                    stages.append(Stage([ld(2 * D, "w2a", (lambda view, s=s: [(view.rearrange("p (j n) -> p j n", j=2), d_w2[l, f, s - 1])])),
                                         ld(2 * D, "w2b", (lambda view, s=s: [(view.rearrange("p (j n) -> p j n", j=2), d_w2[l, f, s])]))], comp_dn2))
            return stages

        def attn_stages(l):
            stages = []
            al = ArenaAlloc(0)
            qT = al.get(2048, BF16)
            kT = al.get(2048, BF16)
            vt = al.get(2048, BF16).rearrange("p (t n) -> p t n", t=8)
            Pb = [al.get(2304, BF16) for _ in range(2)]
            Sb = [al.get(2560, F32) for _ in range(2)]
            rden = [al.get(512, F32) for _ in range(2)]
            sq = [al.get(2048, F32) for _ in range(2)]
            lnv = [al.get(2048, F32) for _ in range(2)]
            rst = [al.get(2048, F32) for _ in range(2)]
            kn32 = [al.get(2048, F32) for _ in range(2)]
            v32 = al.get(4096, F32).rearrange("p (t n) -> p t n", t=8)
            r_q = [Res(), Res()]
            r_k = [Res(), Res()]
            r_v = Res()
            r_P = [Res(), Res()]
            r_Sb = [Res(), Res()]
            r_rden = [Res(), Res()]
            r_sq = [Res(), Res()]
            r_ln = [Res(), Res()]
            r_rst = [Res(), Res()]
            r_kn32 = [Res(), Res()]
            r_v32 = Res()
            cnt_i = [0]

            for c in range(8):
                def comp(stg, c=c):
                    wq = stg.views["wqkv"].rearrange("p (k a n) -> p k a n", k=16, a=3)
                    for which in range(2):
                        dstT = qT if which == 0 else kT
                        rdst = r_q if which == 0 else r_k
                        gcol = qg8[:, l:l + 1] if which == 0 else qkg[:, l, 1:2]
                        for tt in range(NTT):
                            sl = slice(tt * TT, (tt + 1) * TT)
                            i = cnt_i[0] % 2
                            cnt_i[0] += 1
                            bq = i
                            bs = 2 + i
                            for kc in range(16):
                                P.op("pe", (lambda h, kc=kc, bq=bq, which=which: h.matmul(ps[:, bq, :], lhsT=wq[:, kc, which, :], rhs=u[:, kc, sl],
                                                                                         start=(kc == 0), stop=(kc == 15))),
                                     r=[stg.res["wqkv"], R_u[tt]], w=[bank[bq]])
                            P.op("act", (lambda h, i=i, bq=bq: h.activation(out=sq[i], in_=ps[:, bq, :], func=AF.Square)), r=[bank[bq]], w=[r_sq[i]])
                            P.op("pe", (lambda h, i=i, bs=bs: h.matmul(ps[:, bs, :], lhsT=bones32, rhs=sq[i], start=True, stop=True)),
                                 r=[r_sq[i], R_const], w=[bank[bs]])
                            P.op("act", (lambda h, i=i, bs=bs: h.activation(out=lnv[i], in_=ps[:, bs, :], func=AF.Ln, scale=1.0 / 64, bias=EPS)),
                                 r=[bank[bs]], w=[r_ln[i]])
                            P.op("act", (lambda h, i=i: h.activation(out=rst[i], in_=lnv[i], func=AF.Exp, scale=-0.5)), r=[r_ln[i]], w=[r_rst[i]])
                            if which == 1 and not is_s:
                                P.op("dve", (lambda h, i=i, bq=bq: h.scalar_tensor_tensor(out=kn32[i], in0=ps[:, bq, :], scalar=gcol, in1=rst[i],
                                                                                         op0=ALU.mult, op1=ALU.mult)),
                                     r=[bank[bq], r_rst[i], R_const], w=[r_kn32[i]])
                                P.op("act", (lambda h, i=i: h.activation(out=dstT[:, sl], in_=kn32[i], func=AF.Copy)), r=[r_kn32[i]], w=[rdst[tt]])
                                ev = P.dma("sp", (lambda h, i=i, tt=tt: h.dma_start(out=o_nk[l, c, :, tt * TT:(tt + 1) * TT], in_=kn32[i])), r=[r_kn32[i]])
                                arena_reads.append(ev)
                                P.out_events.append(ev)
                            else:
                                P.op("dve", (lambda h, i=i, bq=bq: h.scalar_tensor_tensor(out=dstT[:, sl], in0=ps[:, bq, :], scalar=gcol, in1=rst[i],
                                                                                         op0=ALU.mult, op1=ALU.mult)),
                                     r=[bank[bq], r_rst[i], R_const], w=[rdst[tt]])
                    if cfg.get('attpart', 9) < 2:
                        return
                    for t8 in range(8):
                        bv = 4 + (t8 // 4)
                        co = (t8 % 4) * 128
                        for kc in range(16):
                            P.op("pe", (lambda h, kc=kc, t8=t8, bv=bv, co=co: h.matmul(ps[:, bv, co:co + 128], lhsT=u[:, kc, t8 * 128:(t8 + 1) * 128], rhs=wq[:, kc, 2, :],
                                                                                      start=(kc == 0), stop=(kc == 15))),
                                 r=[stg.res["wqkv"], R_u[t8 // 4]], w=[bank[bv]])
                    for hb in range(2):
                        bv = 4 + hb
                        pvv = ps[:, bv, :].rearrange("p (t n) -> p t n", t=4)
                        if cfg.get('vsub', 9) >= 1:
                            P.op("act", (lambda h, hb=hb, pvv=pvv: h.activation(out=vt[:, 4 * hb:4 * hb + 4, :], in_=pvv, func=AF.Copy)), r=[bank[bv]], w=[r_v])
                        if not is_s and cfg.get('vsub', 9) >= 2:
                            P.op("dve", (lambda h, hb=hb: h.tensor_copy(out=v32[:, 4 * hb:4 * hb + 4, :], in_=pvv)), r=[], w=[r_v32, bank[bv]])
                    if not is_s and cfg.get('vsub', 9) >= 3:
                        ev = P.dma("sp", (lambda h: h.dma_start(out=o_nv[l, c], in_=v32[:])), r=[r_v32])
                        arena_reads.append(ev)
                        P.out_events.append(ev)
                    if cfg.get('attpart', 9) < 3:
                        return
                    def att_body(hh, sq_i, tq, itn):
                        pb = hh * 64
                        t = tq if is_s else sq_i * 2 + tq
                        if is_s:
                            ents = plan[t]
                        else:
                            ents = [(sq_i * 2 + 1, None), (sq_i * 2, None)]
                        nloc = len(ents)
                        nctx = 4 if is_s else 0
                        par = itn % 2
                        b0 = 4 * par
                        Sl = ps[:, b0:b0 + 2, :].rearrange("p a n -> p (a n)")
                        Sc = ps[:, b0 + 2, :]
                        ob = ps[:, b0 + 3, :]
                        qsl = qT[pb:pb + 64, t * 128:(t + 1) * 128]
                        for i, (kt, pat) in enumerate(ents):
                            bki = b0 + (i // 4)
                            P.op("pe", (lambda h, i=i, kt=kt, pat=pat: h.matmul(Sl[:, i * 128:(i + 1) * 128], lhsT=kT[pb:pb + 64, kt * 128:(kt + 1) * 128], rhs=qsl,
                                                                             start=True, stop=(pat is None))),
                                 r=[r_k[kt // 4], r_q[t // 4]], w=[bank[bki]])
                            if pat is not None:
                                pi = pats.index(pat)
                                P.op("pe", (lambda h, i=i, pi=pi: h.matmul(Sl[:, i * 128:(i + 1) * 128], lhsT=maskl[:, pi, :], rhs=ind[:, :],
                                                                         start=False, stop=True)),
                                     r=[R_const], w=[bank[bki]])
                        for j in range(nctx):
                            ck = stg.views["ck"]
                            P.op("pe", (lambda h, j=j, ck=ck: h.matmul(Sc[:, j * 128:(j + 1) * 128], lhsT=ck[pb:pb + 64, j * 128:(j + 1) * 128], rhs=qsl,
                                                                    start=True, stop=True)),
                                 r=[stg.res["ck"], r_q[t // 4]], w=[bank[b0 + 2]])
                        Pt = Pb[par]
                        sbanks = [bank[b0]] + ([bank[b0 + 1]] if nloc > 4 else [])
                        if is_s:
                            bt = stg.views["bias"].rearrange("p (a d c) -> p a d c", a=2, d=16)
                            d0 = 7 - 2 * (ents[0][0] - t)
                            bsl = bt[:, hh, d0:d0 + 2 * nloc, :].rearrange("p d c -> p (d c)")
                            P.op("dve", (lambda h, par=par, nloc=nloc, bsl=bsl, Sl=Sl: h.tensor_tensor(out=Sb[par][:, 0:nloc * 128], in0=Sl[:, 0:nloc * 128], in1=bsl, op=ALU.add)),
                                 r=sbanks + [stg.res["bias"]], w=[r_Sb[par]])
                            P.op("act", (lambda h, par=par, nloc=nloc, Pt=Pt: h.activation(out=Pt[:, 0:nloc * 128], in_=Sb[par][:, 0:nloc * 128], func=AF.Exp)),
                                 r=[r_Sb[par]], w=[r_P[par]])
                            P.op("act", (lambda h, Pt=Pt, Sc=Sc: h.activation(out=Pt[:, 640:1152], in_=Sc, func=AF.Exp)),
                                 r=[bank[b0 + 2]], w=[r_P[par]])
                        else:
                            P.op("act", (lambda h, nloc=nloc, Pt=Pt, Sl=Sl: h.activation(out=Pt[:, 0:nloc * 128], in_=Sl[:, 0:nloc * 128], func=AF.Exp)),
                                 r=sbanks, w=[r_P[par]])
                        yield
                        tiles = [(vt[:, kt, :], Pt[:, i * 128:(i + 1) * 128], r_v) for i, (kt, _) in enumerate(ents)]
                        if is_s:
                            cvv = stg.views["cv"].rearrange("p (j n) -> p j n", j=4)
                            tiles += [(cvv[:, j, :], Pt[:, 640 + j * 128:640 + (j + 1) * 128], stg.res["cv"]) for j in range(4)]
                        nt = len(tiles)
                        for i, (va, pa, rv) in enumerate(tiles):
                            P.op("pe", (lambda h, i=i, va=va, pa=pa, ob=ob, nt=nt: h.matmul(ob[:, 0:128], lhsT=va, rhs=pa, start=(i == 0), stop=(i == nt - 1))),
                                 r=[rv, r_P[par]], w=[bank[b0 + 3]])
                        for i, (va, pa, rv) in enumerate(tiles):
                            P.op("pe", (lambda h, i=i, pa=pa, ob=ob, nt=nt: h.matmul(ob[:, 128:256], lhsT=onesb[:, :], rhs=pa, start=(i == 0), stop=(i == nt - 1))),
                                 r=[r_P[par], R_const], w=[bank[b0 + 3]])
                        P.op("dve", (lambda h, par=par, ob=ob: h.reciprocal(out=rden[par][pb:pb + 64, :], in_=ob[pb:pb + 64, 128:256])),
                             r=[bank[b0 + 3]], w=[r_rden[par]])
                        P.op("dve", (lambda h, par=par, ob=ob, t=t: h.tensor_tensor(out=ao[pb:pb + 64, c, t * 128:(t + 1) * 128], in0=ob[pb:pb + 64, 0:128],
                                                                                 in1=rden[par][pb:pb + 64, :], op=ALU.mult)),
                             r=[bank[b0 + 3], r_rden[par]], w=[R_ao[t // 4]])
                    gens = []
                    for hh in range(2):
                        for sq_i in ([0] if is_s else [0, 1, 2, 3]):
                            for tq in range(8 if is_s else 2):
                                gens.append(att_body(hh, sq_i, tq, len(gens)))
                    next(gens[0])
                    for gi_ in range(len(gens)):
                        if gi_ + 1 < len(gens):
                            next(gens[gi_ + 1])
                        for _ in gens[gi_]:
                            pass
                loads = [ld(16 * 3 * 128, "wqkv", (lambda view, c=c: [(view.rearrange("p (k a n) -> p k a n", k=16, a=3), d_wqkv[l, c])]))]
                if is_s:
                    loads.append(ld(2 * 16 * 64, "bias", (lambda view, c=c: [(view.rearrange("p (a d c) -> p a d c", a=2, d=16), d_bias[l, c])])))
                    loads.append(ld(512, "ck", (lambda view, c=c: [(view, d_ckT[l, c])])))
                    loads.append(ld(512, "cv", (lambda view, c=c: [(view.rearrange("p (j n) -> p j n", j=4), d_cv[l, c])])))
                stages.append(Stage(loads, comp))
            return stages

        CVO_OFF = ARENA - 8192
        PLO_OFF = ARENA - 16384
        cvo = aview(CVO_OFF, 8192, BF16).rearrange("p (c t) -> p c t", c=4)
        plo = aview(PLO_OFF, 8192, BF16).rearrange("p (c t) -> p c t", c=4)
        R_cvo = [Res(), Res()]
        R_plo = [Res(), Res()]
        nseq = 1 if is_s else 4
        Ls = T // nseq

        def conv_stages(l):
            stages = []
            al = ArenaAlloc(0)
            HP = Ls + 30
            hpad = al.get(nseq * HP * 4, F32).rearrange("p (s n) -> p s n", s=nseq)
            c32t = al.get(16384, F32).rearrange("p (c t) -> p c t", c=4)
            sg = [al.get(2048, F32) for _ in range(2)]
            _sq = al.get(2048, F32)
            sq = [_sq, _sq]
            lnv = al.get(2048, F32)
            _rst = al.get(2048, F32)
            rst = [_rst, _rst]
            _yt = al.get(2048, F32)
            yt = [_yt, _yt]
            assert al.off <= CVO_OFF, al.off
            r_hpad = Res()
            r_c32 = [Res() for _ in range(4)]
            r_sg = [Res(), Res()]
            _r = Res()
            r_sq = [_r, _r]
            r_ln = Res()
            _r2 = Res()
            r_rst = [_r2, _r2]
            _r3 = Res()
            r_yt = [_r3, _r3]
            first = [True]

            for c in range(4):
                def comp(stg, c=c):
                    if first[0]:
                        first[0] = False
                        P.op("dve", lambda h: h.memset(hpad, 0.0), w=[r_hpad])
                    wc = stg.views["wc"].rearrange("p (k a n) -> p k a n", k=16, a=2)
                    for tt in range(NTT):
                        sl = slice(tt * TT, (tt + 1) * TT)
                        ba, bg = tt, 2 + tt
                        for which, bk in ((0, ba), (1, bg)):
                            for kc in range(16):
                                P.op("pe", (lambda h, kc=kc, which=which, bk=bk: h.matmul(ps[:, bk, :], lhsT=wc[:, kc, which, :], rhs=u[:, kc, sl],
                                                                                         start=(kc == 0), stop=(kc == 15))),
                                     r=[stg.res["wc"], R_u[tt]], w=[bank[bk]])
                        P.op("act", (lambda h, tt=tt, bg=bg: h.activation(out=sg[tt], in_=ps[:, bg, :], func=AF.Sigmoid)), r=[bank[bg]], w=[r_sg[tt]])
                        if is_s:
                            hv = hpad[:, 0, 15 + tt * TT:15 + (tt + 1) * TT]
                            av = ps[:, ba, :]
                            sv = sg[tt]
                        else:
                            hv = hpad[:, 2 * tt:2 * tt + 2, 15:15 + Ls]
                            av = ps[:, ba, :].rearrange("p (s n) -> p s n", s=2)
                            sv = sg[tt].rearrange("p (s n) -> p s n", s=2)
                        P.op("dve", (lambda h, hv=hv, av=av, sv=sv: h.tensor_tensor(out=hv, in0=av, in1=sv, op=ALU.mult)),
                             r=[bank[ba], r_sg[tt]], w=[r_hpad])
                    acc = c32t[:, c, :].rearrange("p (s n) -> p s n", s=nseq)
                    P.op("dve", (lambda h: h.tensor_scalar(out=acc, in0=hpad[:, :, 0:Ls], scalar1=cdw[:, l, c, 0:1], scalar2=cb[:, l, c:c + 1],
                                                           op0=ALU.mult, op1=ALU.add)), r=[r_hpad, R_const], w=[r_c32[c]])
                    for j in range(1, 31):
                        P.op("dve", (lambda h, j=j: h.scalar_tensor_tensor(out=acc, in0=hpad[:, :, j:j + Ls], scalar=cdw[:, l, c, j:j + 1], in1=acc,
                                                                          op0=ALU.mult, op1=ALU.add)), r=[r_hpad], w=[r_c32[c]])
                    for tt in range(NTT):
                        sl = slice(tt * TT, (tt + 1) * TT)
                        i = (2 * c + tt) % 2
                        P.op("act", (lambda h, i=i: h.activation(out=sq[i], in_=c32t[:, c, sl], func=AF.Square)), r=[r_c32[c]], w=[r_sq[i]])
                        P.op("pe", (lambda h, i=i, tt=tt: h.matmul(ps[:, 4 + tt, :], lhsT=ones32, rhs=sq[i], start=(c == 0), stop=(c == 3))),
                             r=[r_sq[i], R_const], w=[bank[4 + tt]])
                    if c == 3:
                        for tt in range(NTT):
                            sl = slice(tt * TT, (tt + 1) * TT)
                            P.op("act", (lambda h, tt=tt: h.activation(out=lnv, in_=ps[:, 4 + tt, :], func=AF.Ln, scale=1.0 / 512, bias=EPS)),
                                 r=[bank[4 + tt]], w=[r_ln])
                            P.op("act", (lambda h, tt=tt: h.activation(out=rst[tt], in_=lnv, func=AF.Exp, scale=-0.5)), r=[r_ln], w=[r_rst[tt]])
                            for cc in range(4):
                                i = cc % 2
                                P.op("dve", (lambda h, cc=cc, i=i, tt=tt: h.scalar_tensor_tensor(out=yt[i], in0=c32t[:, cc, sl], scalar=cg[:, l, cc:cc + 1], in1=rst[tt],
                                                                                                op0=ALU.mult, op1=ALU.mult)),
                                     r=[r_c32[cc], r_rst[tt]], w=[r_yt[i]])
                                P.op("act", (lambda h, cc=cc, i=i: h.activation(out=cvo[:, cc, sl], in_=yt[i], func=AF.Silu)), r=[r_yt[i]], w=[R_cvo[tt]])
                stages.append(Stage([ld(16 * 2 * 128, "wc", (lambda view, c=c: [(view.rearrange("p (k a n) -> p k a n", k=16, a=2), d_wconv[l, c])]))], comp))
            return stages

        def pool_stages(l):
            stages = []
            al = ArenaAlloc(0)
            ZP = Ls + 16
            zpad = al.get(nseq * ZP * 4, F32).rearrange("p (s n) -> p s n", s=nseq)
            sA = al.get(nseq * ZP * 4, F32).rearrange("p (s n) -> p s n", s=nseq)
            sB = al.get(nseq * ZP * 4, F32).rearrange("p (s n) -> p s n", s=nseq)
            dbf = al.get(2048, BF16)
            ict = al.get(4096, F32)
            r_ict = Res()
            d_inv = d_invS if is_s else d_invP
            assert al.off <= PLO_OFF
            r_z = Res()
            r_sA = Res()
            r_sB = Res()
            r_d = Res()
            r_tb = Res()
            first = [True]
            wins = (2, 4, 8, 16)
            for g in range(4):
                def comp(stg, g=g):
                    if first[0]:
                        first[0] = False
                        P.op("dve", lambda h: h.memset(zpad, 0.0), w=[r_z])
                    wp = stg.views["wp"].rearrange("p (k n) -> p k n", k=16)
                    pw = stg.views["pw"]
                    for tt in range(NTT):
                        sl = slice(tt * TT, (tt + 1) * TT)
                        for kc in range(16):
                            P.op("pe", (lambda h, kc=kc, tt=tt: h.matmul(ps[:, tt, :], lhsT=wp[:, kc, :], rhs=u[:, kc, sl], start=(kc == 0), stop=(kc == 15))),
                                 r=[stg.res["wp"], R_u[tt]], w=[bank[tt]])
                        if is_s:
                            zv = zpad[:, 0, 8 + tt * TT:8 + (tt + 1) * TT]
                            pv = ps[:, tt, :]
                        else:
                            zv = zpad[:, 2 * tt:2 * tt + 2, 8:8 + Ls]
                            pv = ps[:, tt, :].rearrange("p (s n) -> p s n", s=2)
                        P.op("act", (lambda h, zv=zv, pv=pv: h.activation(out=zv, in_=pv, func=AF.Copy)), r=[bank[tt]], w=[r_z])
                    w = wins[g]
                    P.op("dve", lambda h: h.tensor_tensor(out=sA[:, :, 1:ZP], in0=zpad[:, :, 0:ZP - 1], in1=zpad[:, :, 1:ZP], op=ALU.add),
                         r=[r_z], w=[r_sA])
                    cur, rcur, oth, roth = sA, r_sA, sB, r_sB
                    lo, hi = 1, ZP
                    sh = 1
                    for _ in range(g):
                        nlo, nhi = lo + sh, hi - sh
                        P.op("dve", (lambda h, cur=cur, oth=oth, nlo=nlo, nhi=nhi, sh=sh: h.tensor_tensor(out=oth[:, :, nlo:nhi], in0=cur[:, :, nlo - sh:nhi - sh],
                                                                                                      in1=cur[:, :, nlo + sh:nhi + sh], op=ALU.add)),
                             r=[rcur], w=[roth])
                        cur, rcur, oth, roth = oth, roth, cur, rcur
                        lo, hi = nlo, nhi
                        sh *= 2
                    Sv = cur[:, :, 8:8 + Ls]
                    zi = zpad[:, :, 8:8 + Ls]
                    dv = dbf.rearrange("p (s n) -> p s n", s=nseq)
                    P.dma("sp", (lambda h: h.dma_start(out=ict, in_=d_inv[:, g, :])), w=[r_ict],
                          extra=[P.eng[n].last for n in ("pe", "act", "dve")])
                    tmpv = oth[:, :, 8:8 + Ls]
                    P.op("dve", (lambda h: h.tensor_tensor(out=tmpv, in0=Sv, in1=ict.rearrange("p (s n) -> p s n", s=nseq), op=ALU.mult)),
                         r=[rcur, r_ict], w=[roth])
                    P.op("dve", (lambda h: h.tensor_tensor(out=dv, in0=tmpv, in1=zi, op=ALU.subtract)),
                         r=[roth, r_z], w=[r_d])
                    for tt in range(NTT):
                        sl = slice(tt * TT, (tt + 1) * TT)
                        P.op("pe", (lambda h, tt=tt: h.matmul(ps[:, 2 + tt, :], lhsT=pw, rhs=dbf[:, sl], start=True, stop=True)),
                             r=[stg.res["pw"], r_d], w=[bank[2 + tt]])
                        P.op("act", (lambda h, tt=tt: h.activation(out=plo[:, g, sl], in_=ps[:, 2 + tt, :], func=AF.Identity, scale=psc[:, l, g:g + 1], bias=0.0)),
                             r=[bank[2 + tt], R_const], w=[R_plo[tt]])
                loads = [ld(16 * 128, "wp", (lambda view, g=g: [(view.rearrange("p (k n) -> p k n", k=16), d_wpool[l, g])])),
                         ld(128, "pw", (lambda view, g=g: [(view, d_poolw[l, g])]))]
                stages.append(Stage(loads, comp))
            return stages

        def merge_stages(l):
            stages = []
            al = ArenaAlloc(0)
            mg = [al.get(1024, BF16) for _ in range(2)]
            sgt = [al.get(2048, F32) for _ in range(3)]
            mt = [al.get(2048, F32) for _ in range(3)]
            assert al.off <= PLO_OFF
            r_mg = [Res(), Res()]
            r_sgt = [Res() for _ in range(3)]
            r_mt = [Res() for _ in range(3)]
            for n in range(NCH):
                def comp(stg, n=n):
                    wg = stg.views["wg"].rearrange("p (k a n) -> p k a n", k=16, a=3)
                    wba = stg.views["wba"].rearrange("p (k n) -> p k n", k=8)
                    wbb = stg.views["wbb"].rearrange("p (k n) -> p k n", k=4)
                    wbc = stg.views["wbc"].rearrange("p (k n) -> p k n", k=4)
                    wo = stg.views["wo"]
                    brs = [(wba, 8, ao, R_ao, "wba"), (wbb, 4, cvo, R_cvo, "wbb"), (wbc, 4, plo, R_plo, "wbc")]
                    for tt in range(NTT):
                        sl = slice(tt * TT, (tt + 1) * TT)
                        for a in range(3):
                            bg = 2 * a
                            bb = 2 * a + 1
                            for kc in range(16):
                                P.op("pe", (lambda h, kc=kc, a=a, bg=bg: h.matmul(ps[:, bg, :], lhsT=wg[:, kc, a, :], rhs=u[:, kc, sl], start=(kc == 0), stop=(kc == 15))),
                                     r=[stg.res["wg"], R_u[tt]], w=[bank[bg]])
                            wb, nk, src, rsrc, key = brs[a]
                            for kc in range(nk):
                                P.op("pe", (lambda h, kc=kc, wb=wb, src=src, bb=bb, nk=nk: h.matmul(ps[:, bb, :], lhsT=wb[:, kc, :], rhs=src[:, kc, sl],
                                                                                               start=(kc == 0), stop=(kc == nk - 1))),
                                     r=[stg.res[key], rsrc[tt]], w=[bank[bb]])
                            P.op("act", (lambda h, a=a, bg=bg: h.activation(out=sgt[a], in_=ps[:, bg, :], func=AF.Sigmoid)), r=[bank[bg]], w=[r_sgt[a]])
                            P.op("dve", (lambda h, a=a, bb=bb: h.tensor_tensor(out=mt[a], in0=sgt[a], in1=ps[:, bb, :], op=ALU.mult)),
                                 r=[r_sgt[a], bank[bb]], w=[r_mt[a]])
                        P.op("dve", (lambda h: h.tensor_tensor(out=mt[0], in0=mt[0], in1=mt[1], op=ALU.add)), r=[r_mt[1]], w=[r_mt[0]])
                        P.op("dve", (lambda h, tt=tt: h.tensor_tensor(out=mg[tt], in0=mt[0], in1=mt[2], op=ALU.add)), r=[r_mt[0], r_mt[2]], w=[r_mg[tt]])
                    for tt in range(NTT):
                        sl = slice(tt * TT, (tt + 1) * TT)
                        for o in range(NCH):
                            bo = (6, 7, 0, 1, 2, 3, 4, 5)[(o + 8 * tt) % 8]
                            P.op("pe", (lambda h, o=o, bo=bo, tt=tt: h.matmul(ps[:, bo, :], lhsT=wo[:, o * 128:(o + 1) * 128], rhs=mg[tt], start=True, stop=True)),
                                 r=[stg.res["wo"], r_mg[tt]], w=[bank[bo]])
                            P.op("dve", (lambda h, o=o, bo=bo: h.scalar_tensor_tensor(out=x[:, o, sl], in0=ps[:, bo, :], scalar=Gv[:, o:o + 1], in1=x[:, o, sl],
                                                                                     op0=ALU.mult, op1=ALU.add)),
                                 r=[bank[bo], R_AG], w=[R_x[o][tt]])
                loads = [ld(16 * 3 * 128, "wg", (lambda view, n=n: [(view.rearrange("p (k a n) -> p k a n", k=16, a=3), d_wgate[l, n])])),
                         ld(8 * 128, "wba", (lambda view, n=n: [(view.rearrange("p (k n) -> p k n", k=8), d_wbra[l, n])])),
                         ld(4 * 128, "wbb", (lambda view, n=n: [(view.rearrange("p (k n) -> p k n", k=4), d_wbrb[l, n])])),
                         ld(4 * 128, "wbc", (lambda view, n=n: [(view.rearrange("p (k n) -> p k n", k=4), d_wbrc[l, n])])),
                         ld(D, "wo", (lambda view, n=n: [(view, d_wout[l, n])]))]
                stages.append(Stage(loads, comp))
            return stages

        def bar_stage():
            def comp(stg):
                P.barrier(list(arena_reads))
                del arena_reads[:]
            return Stage([], comp)

        stages = []
        for l in range(NL):
            stages.append(norm_stage(l, 0))
            stages += ffn_stages(l, 0)
            stages.append(norm_stage(l, 1))
            stages.append(bar_stage())
            pend = list(mods_extra[l + 1]) if (mods_extra is not None and l + 1 < NL) else []
            for st in attn_stages(l):
                stages.append(st)
                for _ in range(3):
                    if pend:
                        stages.append(pend.pop(0))
            stages.append(bar_stage())
            stages += conv_stages(l)
            stages.append(bar_stage())
            stages += pool_stages(l)
            stages.append(bar_stage())
            for st in merge_stages(l):
                stages.append(st)
                for _ in range(3):
                    if pend:
                        stages.append(pend.pop(0))
            stages += pend
            stages.append(bar_stage())
            stages.append(norm_stage(l, 2))
            stages += ffn_stages(l, 1)
        run_stages(stages[:cfg.get('nst', 100000)])
        for n in range(NCH):
            for tt in range(NTT):
                ev = P.dma("sp", (lambda h, n=n, tt=tt: h.dma_start(out=d_y[n, :, tt * TT:(tt + 1) * TT], in_=x[:, n, tt * TT:(tt + 1) * TT])),
                           r=[R_x[n][tt]])
                P.out_events.append(ev)

    ms = mods_stages()
    if groups:
        run_stages(ms[0])
        for gi_, g in enumerate(groups):
            group_pass(g, ms if gi_ == 0 else None)
    else:
        for m_ in ms:
            run_stages(m_)
    P._waits("sp", P.out_events)
    with nc.Block() as block:
        @block.tensor
        def _(h):
            P.replay("pe", h)

        @block.scalar
        def _(h):
            P.replay("act", h)

        @block.vector
        def _(h):
            P.replay("dve", h)

        @block.gpsimd
        def _(h):
            P.replay("pool", h)

        @block.sync
        def _(h):
            P.replay("sp", h)
    es.close()
    return nc


def _fm(a):
    t, d = a.shape
    return np.ascontiguousarray(a.T.reshape(d // 128, 128, t))


def _vec_fm(v, nchunks):
    lead = v.shape[:-1]
    r = v.reshape(lead + (nchunks, 128))
    return np.ascontiguousarray(np.moveaxis(r, -1, 0))


def prep_shared(inp):
    f = np.float32
    sh = {}
    L = inp["w_mod"].shape[0]
    w_mod = inp["w_mod"]
    sh["wmod"] = np.ascontiguousarray(w_mod.reshape(L, 16, 128, 72, 256).transpose(0, 3, 2, 1, 4))
    bm = _vec_fm(inp["b_mod"], 144)
    sh["bmod"] = np.ascontiguousarray(np.repeat(bm[..., None], 2, axis=-1))
    sh["ng"] = _vec_fm(inp["norm_g"], 16)
    w1 = inp["ffn_w1"]
    w3 = inp["ffn_w3"]
    w2 = inp["ffn_w2"]
    w13 = np.zeros((L, 2, NST, 128, 2, 16, 256), f)
    PADF = NST * 256
    for a, w in ((0, w1), (1, w3)):
        wp = np.zeros((L, 2, D, PADF), f)
        wp[..., :DFF] = w
        w13[:, :, :, :, a] = wp.reshape(L, 2, 16, 128, NST, 256).transpose(0, 1, 4, 3, 2, 5)
        del wp
    sh["w13"] = w13
    w2p = np.zeros((L, 2, PADF, D), f)
    w2p[:, :, :DFF] = w2
    sh["w2"] = np.ascontiguousarray(w2p.reshape(L, 2, NST, 2, 128, D).transpose(0, 1, 2, 4, 3, 5))
    del w2p
    w_in = inp["w_in"].reshape(L, 16, 128, 10752)
    qkv = w_in[..., 0:3072].reshape(L, 16, 128, 3, 8, 128)
    sh["wqkv"] = np.ascontiguousarray(qkv.transpose(0, 4, 2, 1, 3, 5))
    cv = w_in[..., 3072:4096].reshape(L, 16, 128, 2, 4, 128)
    sh["wconv"] = np.ascontiguousarray(cv.transpose(0, 4, 2, 1, 3, 5))
    pl = w_in[..., 4096:4608].reshape(L, 16, 128, 4, 128)
    sh["wpool"] = np.ascontiguousarray(pl.transpose(0, 3, 2, 1, 4))
    gt = w_in[..., 4608:10752].reshape(L, 16, 128, 3, 16, 128)
    sh["wgate"] = np.ascontiguousarray(gt.transpose(0, 4, 2, 1, 3, 5))
    sh["wbra"] = np.ascontiguousarray(inp["w_br_a"].reshape(L, 8, 128, 16, 128).transpose(0, 3, 2, 1, 4))
    sh["wbrb"] = np.ascontiguousarray(inp["w_br_b"].reshape(L, 4, 128, 16, 128).transpose(0, 3, 2, 1, 4))
    sh["wbrc"] = np.ascontiguousarray(inp["w_br_c"].reshape(L, 4, 128, 16, 128).transpose(0, 3, 2, 1, 4))
    sh["wout"] = np.ascontiguousarray(inp["w_out"].reshape(L, 16, 128, D))
    sh["poolw"] = np.ascontiguousarray(inp["pool_w"])
    qk = inp["qk_g"]
    sh["qkg"] = np.ascontiguousarray(np.concatenate([qk, qk], axis=-1).transpose(2, 0, 1))
    sh["cdw"] = np.ascontiguousarray(inp["conv_dw"].reshape(L, 31, 4, 128).transpose(3, 0, 2, 1))
    sh["cb"] = _vec_fm(inp["conv_b"], 4)
    sh["cg"] = _vec_fm(inp["conv_g"], 4)
    sh["psc"] = _vec_fm(inp["pool_scale"], 4)
    rpb = inp["rpb"]
    kc = np.arange(64)[:, None]
    cc = np.arange(64)[None, :]
    cstart = np.clip(cc - 8, 0, 48)
    colok = (kc >= cstart) & (kc < cstart + 16)
    dcidx = np.clip(kc - cc, -15, 15) + 15
    tab = np.full((L, 16, 2, 64, 16, 64), NEG, f)
    for half in range(2):
        for d in range(16):
            dr = (7 - d) if half == 0 else (8 - d)
            if -7 <= dr <= 7:
                m = rpb[:, :, dr + 7, :][:, :, dcidx]
                tab[:, :, half, :, d, :] = np.where(colok[None, None], m, f(NEG))
    tab = tab.reshape(L, 8, 2, 128, 16, 64).transpose(0, 1, 3, 2, 4, 5)
    sh["biasT"] = np.ascontiguousarray(tab)
    pats = mask_patterns()
    ml = np.zeros((2, len(pats), 128), f)
    for pi, pat in enumerate(pats):
        for prp in range(2):
            for krp in range(2):
                if pat[krp][prp]:
                    ml[prp, pi, krp * 64:(krp + 1) * 64] = -32768.0
    sh["maskl"] = ml
    indm = np.zeros((2, 128), f)
    indm[0, 0:64] = 1.0
    indm[1, 64:128] = 1.0
    sh["ind"] = indm
    c32 = np.zeros((128, 256), f)
    c32[:, 0:128] = 1.0
    c32[0:64, 128:192] = 1.0
    c32[64:128, 192:256] = 1.0
    sh["c32"] = c32
    return sh


def inv_table(Ls):
    tab = np.zeros((128, 4, T), np.float32)
    t = np.arange(Ls)
    for g, w in enumerate((2, 4, 8, 16)):
        lo = np.clip(t - w // 2, 0, Ls)
        hi = np.clip(t - w // 2 + w, 0, Ls)
        inv = (1.0 / (hi - lo)).astype(np.float32)
        tab[:, g, :] = np.tile(inv, T // Ls)[None, :]
    return tab


def cnt_table(Ls):
    tab = np.ones((128, 4, 4, 16), np.float32)
    for g, w in enumerate((2, 4, 8, 16)):
        for t in range(w // 2):
            tab[:, g, :, t] = 1.0 / (t + w // 2)
        nr = w // 2 - 1
        for i in range(nr):
            t = Ls - nr + i
            tab[:, g, :, 8 + i] = 1.0 / (Ls - t + w // 2)
    return tab


_NC_CACHE = {}


def kernel(**inp):
    inp = {k: np.asarray(v) for k, v in inp.items()}
    NL = DEBUG["NL"]
    ncores = DEBUG.get("ncores", 8)
    if NL < L:
        for k in ("w_mod", "b_mod", "norm_g", "ffn_w1", "ffn_w3", "ffn_w2", "w_in", "qk_g", "rpb", "w_br_a", "conv_dw",
                  "conv_b", "conv_g", "w_br_b", "pool_w", "pool_scale", "w_br_c", "w_out"):
            inp[k] = inp[k][:NL]
        inp["cache_k"] = inp["cache_k"][:, :NL]
        inp["cache_v"] = inp["cache_v"][:, :NL]
    sh = prep_shared(inp)
    key = (DEBUG["NL"], tuple(DEBUG["groups"]), DEBUG.get("stop"), DEBUG.get("nmods"), DEBUG.get("nst"), DEBUG.get("attpart"), DEBUG.get("vsub"))
    import time as _t
    _t0 = _t.time()
    if key not in _NC_CACHE:
        _NC_CACHE[key] = build_nc(DEBUG)
    nc = _NC_CACHE[key]
    xp = inp["x_prompt"]
    xs = inp["x_sample"]
    ck = inp["cache_k"]
    cv = inp["cache_v"]
    in_maps = []
    for c in range(ncores):
        b = c // 2
        m = dict(sh)
        m["xs"] = _fm(xs[b])
        m["xp"] = _fm(xp[4 * c:4 * c + 4].reshape(T, D))
        cvec = np.stack([inp["c"][b], inp["c_ctx"]], axis=-1)
        m["cvec"] = np.ascontiguousarray(cvec.reshape(16, 128, 2).transpose(1, 0, 2))
        ckb = inp["cache_k"][b]
        m["ckT"] = np.ascontiguousarray(ckb.reshape(NL, 512, 8, 128).transpose(0, 2, 3, 1))
        cvb = inp["cache_v"][b]
        m["cvp"] = np.ascontiguousarray(cvb.reshape(NL, 4, 128, 8, 128).transpose(0, 3, 2, 1, 4))
        m["cnt"] = np.concatenate([cnt_table(1024)[:, :, 0:0], cnt_table(256)], axis=2) if False else cnt_table(256)
        m["invS"] = inv_table(1024)
        m["invP"] = inv_table(256)
        in_maps.append(m)
    print('host prep s', _t.time() - _t0, flush=True)
    res = run_bass_kernel_spmd(nc, in_maps, core_ids=list(range(ncores)))
    print('run s', _t.time() - _t0, flush=True)
    outs = res.results
    B, S = 32, 256
    yp = np.zeros((B, S, D), np.float32)
    ys = np.zeros((4, 1024, D), np.float32)
    nk = np.zeros((B, L, S, 16, 64), np.float32)
    nv = np.zeros((B, L, S, 16, 64), np.float32)
    for c in range(ncores):
        o = outs[c]
        yp[4 * c:4 * c + 4] = o["yp"].reshape(D, T).T.reshape(4, S, D)
        if c % 2 == 0:
            ys[c // 2] = o["ys"].reshape(D, T).T
        k = o["nk"].transpose(3, 0, 1, 2).reshape(4, S, NL, 16, 64)
        nk[4 * c:4 * c + 4, :NL] = k.transpose(0, 2, 1, 3, 4)
        v = o["nv"].transpose(3, 2, 0, 1, 4).reshape(T, NL, 1024).reshape(4, S, NL, 16, 64)
        nv[4 * c:4 * c + 4, :NL] = v.transpose(0, 2, 1, 3, 4)
    return (yp, ys, nk, nv)
```
